# Optimizing a Trainium2 kernel written in Bass

```python
import jax, jax.numpy as jnp
from jax import lax
import numpy as np

D_MODEL = 4096
BATCH = 1
SEQ = 16384
DEPTH = 4

MEM_LEN = 256
CHUNK = 64
EPS = 1e-6
MIX_V = D_MODEL // 4
GLA_HEADS = 4
GLA_DV = MIX_V // GLA_HEADS
GLA_DK = GLA_DV // 2
GLA_QK = GLA_HEADS * GLA_DK
GLA_V = GLA_HEADS * GLA_DV
GLA_RANK = 16
GLA_TAU = 16.0
ML_HEADS = 4
ML_DV = MIX_V // ML_HEADS
ML_DK = ML_DV // 2
ML_QK = ML_HEADS * ML_DK
ML_V = ML_HEADS * ML_DV
CONV_W = 4
X_HEADS = 4
X_DH = MIX_V // X_HEADS
X_W = X_HEADS * X_DH
N_BRANCH = 3
GATE_RANK = 256
D_FF = 4 * D_MODEL

IN_SPLITS = (GLA_QK, GLA_QK, GLA_V, GLA_V, GLA_RANK,
             2 * ML_QK, ML_V, ML_V, ML_HEADS, ML_HEADS,
             X_W, GATE_RANK)
IN_WIDTH = sum(IN_SPLITS)

kernel_name = "hybrid_gla_mlstm_memory_trunk"


def split_columns(z):
    idx, acc = [], 0
    for w in IN_SPLITS[:-1]:
        acc += w
        idx.append(acc)
    return jnp.split(z, idx, axis=-1)


def rmsnorm(x, g):
    xf = x.astype(jnp.float32)
    xf = xf * lax.rsqrt(jnp.mean(xf * xf, axis=-1, keepdims=True) + EPS)
    return xf.astype(x.dtype) * g


def head_rmsnorm(o, heads, g):
    B, S, W = o.shape
    of = o.astype(jnp.float32).reshape(B, S, heads, W // heads)
    of = of * lax.rsqrt(jnp.mean(of * of, axis=-1, keepdims=True) + EPS)
    return of.reshape(B, S, W).astype(o.dtype) * g


def to_chunks(t, heads):
    B, S, W = t.shape
    return t.reshape(B, S // CHUNK, CHUNK, heads, W // heads).transpose(0, 3, 1, 2, 4)


def gate_chunks(t):
    B, S, H = t.shape
    return t.reshape(B, S // CHUNK, CHUNK, H).transpose(0, 3, 1, 2)


def from_chunks(t):
    B, H, N, C, d = t.shape
    return t.transpose(0, 2, 3, 1, 4).reshape(B, N * C, H * d)


def causal_dwconv(u, w, b):
    y = lax.conv_general_dilated(
        u, w[:, None, :].astype(u.dtype), window_strides=(1,),
        padding=[(CONV_W - 1, 0)], dimension_numbers=("NWC", "WIO", "NWC"),
        feature_group_count=u.shape[-1])
    return y + b.astype(u.dtype)


def gla_mixer(q, k, v, log_a):
    q = to_chunks(q, GLA_HEADS) * (GLA_DK ** -0.5)
    k = to_chunks(k, GLA_HEADS)
    v = to_chunks(v, GLA_HEADS)
    b = jnp.cumsum(to_chunks(log_a, GLA_HEADS), axis=3)
    causal = jnp.tril(jnp.ones((CHUNK, CHUNK), dtype=bool))
    q_d = q * jnp.exp(b)
    att = jnp.einsum("bhnck,bhnsk->bhncs", q_d, k * jnp.exp(-b))
    att = jnp.where(causal, att, 0.0)
    o_intra = jnp.einsum("bhncs,bhnsv->bhncv", att, v)
    b_last = b[:, :, :, -1:, :]
    d_state = jnp.einsum("bhnck,bhncv->bhnkv", k * jnp.exp(b_last - b), v)
    decay = jnp.exp(b_last[:, :, :, 0, :])

    def step(S, inp):
        dec, ds = inp
        return dec[..., None] * S + ds, S

    B, H = q.shape[0], q.shape[1]
    S0 = jnp.zeros((B, H, GLA_DK, GLA_DV), jnp.float32)
    _, S_prev = lax.scan(step, S0, (jnp.moveaxis(decay, 2, 0), jnp.moveaxis(d_state, 2, 0)))
    S_prev = jnp.moveaxis(S_prev, 0, 2)
    o_inter = jnp.einsum("bhnck,bhnkv->bhncv", q_d, S_prev)
    return from_chunks(o_intra + o_inter)


def mlstm_mixer(q, k, v, i_pre, f_pre):
    q = to_chunks(q, ML_HEADS)
    k = to_chunks(k, ML_HEADS) * (ML_DK ** -0.5)
    v = to_chunks(v, ML_HEADS)
    log_i = gate_chunks(i_pre)
    F = jnp.cumsum(gate_chunks(jax.nn.log_sigmoid(f_pre)), axis=3)
    causal = jnp.tril(jnp.ones((CHUNK, CHUNK), dtype=bool))
    L = jnp.where(causal, F[..., :, None] - F[..., None, :] + log_i[..., None, :], -jnp.inf)
    F_last = F[..., -1]
    G = F_last[..., None] - F + log_i
    g = jnp.max(G, axis=-1)
    wG = jnp.exp(G - g[..., None])
    dC = jnp.einsum("bhnc,bhnck,bhncv->bhnkv", wG, k, v)
    dn = jnp.einsum("bhnc,bhnck->bhnk", wG, k)

    def step(carry, inp):
        C, n, m = carry
        fl, gl, dc, dnn = inp
        m_new = jnp.maximum(fl + m, gl)
        a = jnp.exp(fl + m - m_new)
        bb = jnp.exp(gl - m_new)
        C_new = a[..., None, None] * C + bb[..., None, None] * dc
        n_new = a[..., None] * n + bb[..., None] * dnn
        return (C_new, n_new, m_new), (C, n, m)

    B, H = q.shape[0], q.shape[1]
    init = (jnp.zeros((B, H, ML_DK, ML_DV), jnp.float32),
            jnp.zeros((B, H, ML_DK), jnp.float32),
            jnp.full((B, H), -1e30, jnp.float32))
    xs = (jnp.moveaxis(F_last, 2, 0), jnp.moveaxis(g, 2, 0),
          jnp.moveaxis(dC, 2, 0), jnp.moveaxis(dn, 2, 0))
    _, (C_prev, n_prev, m_prev) = lax.scan(step, init, xs)
    C_prev = jnp.moveaxis(C_prev, 0, 2)
    n_prev = jnp.moveaxis(n_prev, 0, 2)
    m_prev = jnp.moveaxis(m_prev, 0, 2)

    m_inter = F + m_prev[..., None]
    m = jnp.maximum(jnp.max(L, axis=-1), m_inter)
    s = jnp.einsum("bhnck,bhnsk->bhncs", q, k) * jnp.exp(L - m[..., None])
    w_inter = jnp.exp(m_inter - m)
    num = jnp.einsum("bhncs,bhnsv->bhncv", s, v) + \
        w_inter[..., None] * jnp.einsum("bhnck,bhnkv->bhncv", q, C_prev)
    den = jnp.sum(s, axis=-1) + w_inter * jnp.einsum("bhnck,bhnk->bhnc", q, n_prev)
    h = num / jnp.maximum(jnp.abs(den), jnp.exp(-m))[..., None]
    return from_chunks(h)


def memory_attention(q, km, vm):
    B, S, _ = q.shape
    M = km.shape[1]
    qh = q.reshape(B, S, X_HEADS, X_DH)
    kh = km.reshape(B, M, X_HEADS, X_DH)
    vh = vm.reshape(B, M, X_HEADS, X_DH)
    s = jnp.einsum("bshd,bmhd->bhsm", qh, kh).astype(jnp.float32) * (X_DH ** -0.5)
    p = jax.nn.softmax(s, axis=-1).astype(vh.dtype)
    return jnp.einsum("bhsm,bmhd->bshd", p, vh).reshape(B, S, X_W)


def setup_inputs(seed: int = 0) -> dict:
    key = jax.random.key(seed)
    ks = jax.random.split(key, 24)
    f32 = jnp.float32

    def nrm(k, shape, scale):
        return jax.random.normal(k, shape, f32) * scale

    res_scale = (2 * DEPTH) ** -0.5
    b_f = jnp.broadcast_to(jnp.linspace(3.0, 6.0, ML_HEADS, dtype=f32), (DEPTH, ML_HEADS))
    return {
        "x": nrm(ks[0], (BATCH, SEQ, D_MODEL), 1.0),
        "mem": nrm(ks[1], (BATCH, MEM_LEN, D_MODEL), 1.0),
        "norm_mix": 1.0 + nrm(ks[2], (DEPTH, D_MODEL), 0.01),
        "w_in": nrm(ks[3], (DEPTH, D_MODEL, IN_WIDTH), D_MODEL ** -0.5),
        "w_gla_a": nrm(ks[4], (DEPTH, GLA_RANK, GLA_QK), GLA_RANK ** -0.5),
        "b_gla_a": nrm(ks[5], (DEPTH, GLA_QK), 0.1),
        "gla_norm": 1.0 + nrm(ks[6], (DEPTH, GLA_V), 0.01),
        "conv_w": nrm(ks[7], (DEPTH, CONV_W, 2 * ML_QK), CONV_W ** -0.5),
        "conv_b": nrm(ks[8], (DEPTH, 2 * ML_QK), 0.01),
        "b_ml_i": nrm(ks[9], (DEPTH, ML_HEADS), 0.1),
        "b_ml_f": b_f + nrm(ks[10], (DEPTH, ML_HEADS), 0.1),
        "ml_norm": 1.0 + nrm(ks[11], (DEPTH, ML_V), 0.01),
        "mem_norm": 1.0 + nrm(ks[12], (D_MODEL,), 0.01),
        "w_mem_k": nrm(ks[13], (DEPTH, D_MODEL, X_W), D_MODEL ** -0.5),
        "w_mem_v": nrm(ks[14], (DEPTH, D_MODEL, X_W), D_MODEL ** -0.5),
        "w_branch": nrm(ks[15], (DEPTH, N_BRANCH, MIX_V, D_MODEL), MIX_V ** -0.5),
        "w_gate": nrm(ks[16], (DEPTH, N_BRANCH, GATE_RANK, D_MODEL), GATE_RANK ** -0.5),
        "b_gate": nrm(ks[17], (DEPTH, N_BRANCH, D_MODEL), 0.01),
        "w_out": nrm(ks[18], (DEPTH, D_MODEL, D_MODEL), D_MODEL ** -0.5 * res_scale),
        "norm_ffn": 1.0 + nrm(ks[19], (DEPTH, D_MODEL), 0.01),
        "w_ff1": nrm(ks[20], (DEPTH, D_MODEL, D_FF), D_MODEL ** -0.5),
        "w_ff2": nrm(ks[21], (DEPTH, D_FF, D_MODEL), D_FF ** -0.5 * res_scale),
        "final_norm": 1.0 + nrm(ks[22], (D_MODEL,), 0.01),
    }


def reference(x, mem, norm_mix, w_in, w_gla_a, b_gla_a, gla_norm, conv_w, conv_b,
              b_ml_i, b_ml_f, ml_norm, mem_norm, w_mem_k, w_mem_v, w_branch,
              w_gate, b_gate, w_out, norm_ffn, w_ff1, w_ff2, final_norm):
    f32 = jnp.float32
    memn = rmsnorm(mem, mem_norm)
    for l in range(DEPTH):
        h = rmsnorm(x, norm_mix[l])
        z = h @ w_in[l]
        (gq, gk, gv, gg, ga, mqk, mv, mo, mi, mf, xq, gate_lat) = split_columns(z)

        log_a = jax.nn.log_sigmoid((ga @ w_gla_a[l] + b_gla_a[l]).astype(f32)) / GLA_TAU
        o_gla = gla_mixer(gq.astype(f32), gk.astype(f32), gv.astype(f32), log_a).astype(x.dtype)
        o_gla = head_rmsnorm(o_gla, GLA_HEADS, gla_norm[l]) * jax.nn.silu(gg)

        mqk = jax.nn.silu(causal_dwconv(mqk, conv_w[l], conv_b[l]))
        mq, mk = jnp.split(mqk, 2, axis=-1)
        h_ml = mlstm_mixer(mq.astype(f32), mk.astype(f32), mv.astype(f32),
                           (mi + b_ml_i[l]).astype(f32), (mf + b_ml_f[l]).astype(f32))
        o_ml = jax.nn.sigmoid(mo) * h_ml.astype(x.dtype)
        o_ml = head_rmsnorm(o_ml, ML_HEADS, ml_norm[l])

        o_mem = memory_attention(xq, memn @ w_mem_k[l], memn @ w_mem_v[l])

        y = None
        for j, o_b in enumerate((o_gla, o_ml, o_mem)):
            gate = jax.nn.sigmoid(gate_lat @ w_gate[l, j] + b_gate[l, j])
            term = gate * (o_b @ w_branch[l, j])
            y = term if j == 0 else y + term
        x = x + y @ w_out[l]

        h = rmsnorm(x, norm_ffn[l])
        x = x + jnp.square(jax.nn.relu(h @ w_ff1[l])) @ w_ff2[l]
    return rmsnorm(x, final_norm)
```

```python
import math
from contextlib import ExitStack
import numpy as np
import concourse.bass as bass
import concourse.mybir as mybir
from concourse.bass_utils import run_bass_kernel_spmd

F32 = mybir.dt.float32
BF16 = mybir.dt.bfloat16
AF = mybir.ActivationFunctionType
ALU = mybir.AluOpType

NH = 4
DK = 128
DV = 256
MIXV = 1024
RANK = 16
MEM = 256
IN_W = 7448
EPS = 1e-6
BLK = 128
NEG = -1e30

ENGS = ("pe", "act", "dve", "pool", "sp")


class Buf:
    __slots__ = ("name", "last_w", "readers", "sem", "dcount", "last_dma")

    def __init__(self, name):
        self.name = name
        self.last_w = None
        self.readers = []
        self.sem = None
        self.dcount = 0
        self.last_dma = None


class V:
    __slots__ = ("ap", "b")

    def __init__(self, ap, b):
        self.ap = ap
        self.b = b


class Tl:
    def __init__(self, t, b):
        self.t = t
        self.b = b

    def __getitem__(self, idx):
        return V(self.t[idx], self.b)


class Op:
    __slots__ = ("eng", "fn", "deps", "sig", "sig_idx", "waits", "dma_buf", "dma_val", "inc", "extra", "gidx")


class Prog:
    def __init__(self, nc, same_engine_sync=True):
        self.nc = nc
        self.ops = {e: [] for e in ENGS}
        self.last = {e: None for e in ENGS}
        self.pending = {e: None for e in ENGS}
        self.n = 0
        self.scopes = []
        self.sem_free = []
        self.sem_all = []
        self.live_dma_bufs = []
        self.top = ExitStack()
        self.ses = same_engine_sync
        self.uid = 0

    def _stack(self):
        return self.scopes[-1][0] if self.scopes else self.top

    def sb(self, name, shape, dt):
        self.uid += 1
        t = self._stack().enter_context(self.nc.sbuf_tensor(f"{name}_{self.uid}", list(shape), dt))
        b = Buf(name)
        if self.scopes:
            self.scopes[-1][1].append(b)
        return Tl(t, b)

    def ps(self, name, shape, dt):
        t = self.top.enter_context(self.nc.psum_tensor(name, list(shape), dt))
        return Tl(t, Buf(name))

    def phase(self):
        prog = self

        class _Ph:
            def __enter__(s):
                prog.scopes.append((ExitStack(), []))
                return s

            def __exit__(s, *a):
                prog.barrier()
                es, bufs = prog.scopes.pop()
                for b in bufs:
                    if b.sem is not None:
                        prog.sem_free.append(b.sem)
                        if b in prog.live_dma_bufs:
                            prog.live_dma_bufs.remove(b)
                        b.sem = None
                es.close()
                return False
        return _Ph()

    def _get_sem(self, b):
        if b.sem is None:
            if self.sem_free:
                b.sem = self.sem_free.pop()
            else:
                s = self.top.enter_context(self.nc.semaphore(f"dq{len(self.sem_all)}"))
                b.sem = [s, 0]
                self.sem_all.append(b.sem)
            self.live_dma_bufs.append(b)
        return b.sem

    def add(self, eng, fn, r=(), w=(), dma_buf=None, inc=None, extra=()):
        op = Op()
        op.eng = eng
        op.fn = fn
        op.sig = False
        op.sig_idx = None
        op.waits = None
        op.dma_buf = dma_buf
        op.dma_val = None
        op.inc = inc
        op.extra = list(extra)
        op.gidx = self.n
        self.n += 1
        deps = set()
        wset = [b for b in w if b is not None]
        rset = [b for b in r if b is not None and b not in wset]
        for b in rset:
            if b.last_w is not None:
                deps.add(b.last_w)
        for b in wset:
            if b.last_w is not None:
                deps.add(b.last_w)
            deps.update(b.readers)
        if self.pending[eng] is not None:
            deps.update(self.pending[eng])
            self.pending[eng] = None
        for b in rset:
            if dma_buf is None:
                b.readers = [o for o in b.readers if o.eng != eng or o.dma_buf is not None]
            b.readers.append(op)
        for b in wset:
            b.last_w = op
            b.readers = []
        if dma_buf is not None:
            sem = self._get_sem(dma_buf)
            sem[1] += 16
            op.dma_val = sem[1]
            op.inc = (sem[0], 16)
            dma_buf.last_dma = op
        op.deps = deps
        self.ops[eng].append(op)
        if dma_buf is None and inc is None and fn is not None:
            self.last[eng] = op
        return op

    def barrier(self):
        deps = [o for o in self.last.values() if o is not None]
        for b in self.live_dma_bufs:
            if b.last_dma is not None:
                deps.append(b.last_dma)
        for e in ENGS:
            cur = self.pending[e]
            self.pending[e] = list(deps) if cur is None else list(set(cur) | set(deps))

    def finalize(self, eng_sems):
        for e in ENGS:
            for op in self.ops[e]:
                for d in op.deps:
                    if d.dma_buf is None and d.inc is None:
                        if d.eng == "pe" and op.eng == "pe":
                            continue
                        if d.eng == op.eng and not self.ses:
                            continue
                        d.sig = True
        for e in ENGS:
            c = 0
            for op in self.ops[e]:
                if op.sig:
                    c += 1
                    op.sig_idx = c
        for e in ENGS:
            known = {}
            for op in self.ops[e]:
                want = {}
                for d in op.deps:
                    if d.dma_buf is not None:
                        key, val = d.inc[0], d.dma_val
                    elif d.inc is not None:
                        continue
                    else:
                        if d.eng == "pe" and op.eng == "pe":
                            continue
                        if d.eng == op.eng and not self.ses:
                            continue
                        key, val = eng_sems[d.eng], d.sig_idx
                    if val > want.get(id(key), (None, 0))[1]:
                        want[id(key)] = (key, val)
                for (key, val) in op.extra:
                    if val > want.get(id(key), (None, 0))[1]:
                        want[id(key)] = (key, val)
                ws = []
                for k, (key, val) in want.items():
                    if known.get(k, 0) < val:
                        known[k] = val
                        ws.append((key, val))
                op.waits = ws

    def emit(self, eng, e, eng_sems):
        for op in self.ops[eng]:
            for (s, v) in op.waits:
                e.wait_ge(s, v)
            if op.fn is None:
                continue
            ins = op.fn(e)
            if op.inc is not None:
                ins.then_inc(op.inc[0], op.inc[1])
            elif op.sig:
                ins.then_inc(eng_sems[eng], 1)

    def mm(self, out, lhsT, rhs, start=True, stop=True):
        return self.add("pe", lambda e: e.matmul(out.ap, lhsT.ap, rhs.ap, start=start, stop=stop),
                        r=[lhsT.b, rhs.b], w=[out.b])

    def tr(self, out, in_, ident):
        return self.add("pe", lambda e: e.transpose(out.ap, in_.ap, ident.ap), r=[in_.b, ident.b], w=[out.b])

    def act(self, out, in_, func, bias=None, scale=None):
        rb = [in_.b]
        kw = {}
        if bias is not None:
            if isinstance(bias, V):
                rb.append(bias.b)
                kw["bias"] = bias.ap
            else:
                kw["bias"] = float(bias)
        if scale is not None:
            if isinstance(scale, V):
                rb.append(scale.b)
                kw["scale"] = scale.ap
            else:
                kw["scale"] = float(scale)
        return self.add("act", lambda e: e.activation(out.ap, in_.ap, func, **kw), r=rb, w=[out.b])

    def tt(self, out, a, b, op, eng="dve"):
        return self.add(eng, lambda e: e.tensor_tensor(out.ap, a.ap, b.ap, op), r=[a.b, b.b], w=[out.b])

    def ts(self, out, a, s1, op0, s2=None, op1=None, eng="dve"):
        rb = [a.b]
        x1 = s1
        if isinstance(s1, V):
            rb.append(s1.b)
            x1 = s1.ap
        x2 = s2
        if isinstance(s2, V):
            rb.append(s2.b)
            x2 = s2.ap
        if op1 is None:
            return self.add(eng, lambda e: e.tensor_scalar(out.ap, a.ap, x1, None, op0), r=rb, w=[out.b])
        return self.add(eng, lambda e: e.tensor_scalar(out.ap, a.ap, x1, x2, op0, op1), r=rb, w=[out.b])

    def stt(self, out, a, s, b, op0, op1):
        rb = [a.b, b.b]
        x = s
        if isinstance(s, V):
            rb.append(s.b)
            x = s.ap
        return self.add("dve", lambda e: e.scalar_tensor_tensor(out.ap, a.ap, x, b.ap, op0, op1), r=rb, w=[out.b])

    def cp(self, out, in_, eng="dve"):
        if eng == "act":
            return self.add("act", lambda e: e.activation(out.ap, in_.ap, AF.Copy), r=[in_.b], w=[out.b])
        return self.add(eng, lambda e: e.tensor_copy(out.ap, in_.ap), r=[in_.b], w=[out.b])

    def memset(self, out, val, eng="dve"):
        return self.add(eng, lambda e: e.memset(out.ap, val), w=[out.b])

    def recip(self, out, in_):
        return self.add("dve", lambda e: e.reciprocal(out.ap, in_.ap), r=[in_.b], w=[out.b])

    def scan(self, out, d0, d1, init, op0, op1):
        rb = [d0.b, d1.b]
        x = init
        if isinstance(init, V):
            rb.append(init.b)
            x = init.ap
        return self.add("dve", lambda e: e.tensor_tensor_scan(out.ap, d0.ap, d1.ap, x, op0, op1), r=rb, w=[out.b])

    def dma(self, out, in_, extra=(), q="sp"):
        if out.b is not None:
            return self.add(q, lambda e: e.dma_start(out=out.ap, in_=in_.ap), r=[in_.b], w=[out.b],
                            dma_buf=out.b, extra=extra)
        return self.add(q, lambda e: e.dma_start(out=out.ap, in_=in_.ap), r=[in_.b], w=[],
                        dma_buf=in_.b, extra=extra)


def build_program(NC, T, D, DFF, L, debug=False):
    KC = D // 128
    FC = DFF // 128
    TT = min(512, T)
    NT = T // TT
    NB = T // BLK
    BPT = TT // BLK
    SEG = 16 if FC >= 16 else FC
    NSEG = FC // SEG
    nc = bass.Bass("TRN2", target_bir_lowering=False)
    P = Prog(nc)

    def din(name, shape, dt=F32):
        return nc.dram_tensor(name, list(shape), dt, kind="ExternalInput").ap()

    def dscr(name, shape, dt):
        return nc.dram_tensor(name, list(shape), dt, kind="Internal").ap()

    x_in = din("xT", [D, T])
    memT_in = din("memT", [D, MEM])
    CO = {}
    ncol = 0

    def col(name, n):
        nonlocal ncol
        CO[name] = ncol
        ncol += n
    col("norm_mix", L * KC)
    col("norm_ffn", L * KC)
    col("final_norm", KC)
    col("mem_norm", KC)
    col("b_gate", L * 3 * KC)
    col("gla_norm", L * 8)
    col("ml_norm", L * 8)
    col("nb_gla_a", L * 4)
    col("conv_w", L * 4 * 8)
    col("conv_b", L * 8)
    col("sel", 8)
    col("selprev", 8)
    cols_in = din("cols", [128, ncol])
    consts_in = din("consts", [128, 3 * 128])
    g4_in = din("g4", [4, 4 * 128 + 4 * L + 8])
    g4m_in = din("g4m", [4, 2 * T])
    wga_in = din("wga", [RANK, L * 512])
    WNAMES = ["w_in", "w_mk", "w_mv", "w_bg", "w_out", "w_ff1", "w_ff2"]
    WSHAPE = {"w_in": (D, IN_W), "w_mk": (D, MIXV), "w_mv": (D, MIXV), "w_bg": (3 * MIXV + 3 * 256, D),
              "w_out": (D, D), "w_ff1": (D, DFF), "w_ff2": (DFF, D)}
    w_sh_in = {}
    w_sh_bf = {}
    w_full = {}
    for l in range(L):
        for wn in WNAMES:
            R, C = WSHAPE[wn]
            w_sh_in[(l, wn)] = din(f"{wn}_{l}", [R // NC, C])
            w_sh_bf[(l, wn)] = dscr(f"{wn}_sb_{l}", [R // NC, C], BF16)
            w_full[(l, wn)] = dscr(f"{wn}_f_{l}", [R, C], BF16)
    okind = "ExternalOutput"
    outT = nc.dram_tensor("outT", [D, T], F32, kind=okind).ap()

    dk_ = "ExternalOutput" if debug else "Internal"

    def dbg(name, shape, dt):
        return nc.dram_tensor(name, list(shape), dt, kind=dk_).ap()
    xs = dscr("x_scr", [D, T], F32)
    ZF = 5400
    zf = dbg("zf", [ZF, T], F32)
    zc = dbg("zc", [1024, T], F32)
    vtok = dbg("vtok", [T, 2048], BF16)
    obr = dbg("obr", [3 * MIXV, T], BF16)
    halo_src = dscr("halo_src", [1024, 8], F32)
    halo_all = dscr("halo_all", [NC * 1024, 8], F32)
    XR = 1152
    XW = 264
    xch_src = dscr("xch_src", [XR, XW], F32)
    xch_all = dscr("xch_all", [NC * XR, XW], F32)
    rg = [list(range(NC))]

    es_ = P.top
    eng_sems = {e: es_.enter_context(nc.semaphore(f"eng_{e}")) for e in ENGS}
    sem_cast = es_.enter_context(nc.semaphore("wcast"))
    sem_ag = es_.enter_context(nc.semaphore("wag"))
    sem_cc = es_.enter_context(nc.semaphore("cc"))
    ag_count = [0]
    cast_count = [0]
    cc_count = [0]
    w_ready = {}

    colsb = P.sb("cols", [128, ncol], F32)
    cst = P.sb("consts", [128, 384], F32)
    maskT = cst[:, 0:128]
    ones32 = cst[:, 256:384]
    identb = P.sb("identb", [128, 128], BF16)
    onesb = P.sb("onesb", [128, 128], BF16)
    g4 = P.sb("g4", [4, 4 * 128 + 4 * L + 8], F32)
    G4_B = 512
    G4_SEL = G4_B + 4 * L
    memn_d = dscr("memn_d", [D, MEM], BF16)
    epsc = P.sb("epsc", [128, 1], F32)
    PSB = [P.ps(f"ps{i}", [128, 512], F32) for i in range(7)]
    PST = P.ps("pst", [128, 1024], BF16)

    def C_(name, i=0):
        j = CO[name] + i
        return colsb[:, j:j + 1]

    def prep_weights(l):
        for wn in WNAMES:
            src, sb_, full = w_sh_in[(l, wn)], w_sh_bf[(l, wn)], w_full[(l, wn)]
            cast_count[0] += 16
            P.add("pool", (lambda e, o=sb_, i=src: e.dma_start(out=o, in_=i)), inc=(sem_cast, 16))
            ag_count[0] += 1
            P.add("pool", (lambda e, o=full, i=sb_: e.collective_compute(
                "AllGather", ALU.bypass, replica_groups=rg, ins=[i], outs=[o])),
                inc=(sem_ag, 1), extra=[(sem_cast, cast_count[0])])
            P.add("pool", None, extra=[(sem_ag, ag_count[0])])
            w_ready[(l, wn)] = ag_count[0]

    def collective_ag(src_ap, dst_ap):
        P.barrier()
        cc_count[0] += 1
        P.add("pool", (lambda e: e.collective_compute("AllGather", ALU.bypass, replica_groups=rg,
                                                      ins=[src_ap], outs=[dst_ap])), inc=(sem_cc, 1))
        for e in ENGS:
            P.add(e, None, extra=[(sem_cc, cc_count[0])])

    P.dma(colsb[:, :], V(cols_in, None))
    P.dma(cst[:, :], V(consts_in, None))
    P.dma(g4[:, :], V(g4_in, None))
    P.cp(identb[:, :], cst[:, 128:256])
    P.memset(onesb[:, :], 1.0)
    P.memset(epsc[:, :], EPS)

    prep_weights(0)

    wslot_n = [0]

    def rms_scale(ps, tmp_std, rstd, n, width):
        P.act(tmp_std, ps, AF.Sqrt, bias=epsc[:, 0:1], scale=1.0 / n)
        P.recip(rstd, tmp_std)

    def load_panel(wslots, l, wn, r0, nk, c0, pw):
        s = wslots[wslot_n[0] % len(wslots)]
        wslot_n[0] += 1
        full = w_full[(l, wn)]
        src = full[r0:r0 + nk * 128, c0:c0 + pw].rearrange("(k p) n -> p k n", p=128)
        view = s.t[:, 0:nk * pw].rearrange("p (k n) -> p k n", n=pw)
        P.dma(V(view, s.b), V(src, None), extra=[(sem_ag, w_ready[(l, wn)])])
        return V(view, s.b)

    with P.phase():
        mt = P.sb("mt", [128, KC, MEM], F32)
        acc = P.sb("macc", [128, MEM], F32)
        sq = P.sb("msq", [128, MEM], F32)
        std = P.sb("mstd", [128, MEM], F32)
        rstd = P.sb("mrstd", [128, MEM], F32)
        P.dma(mt[:, :, :], V(memT_in.rearrange("(k p) m -> p k m", p=128), None))
        for k in range(KC):
            if k == 0:
                P.tt(acc[:, :], mt[:, k, :], mt[:, k, :], ALU.mult)
            else:
                P.tt(sq[:, :], mt[:, k, :], mt[:, k, :], ALU.mult)
                P.tt(acc[:, :], acc[:, :], sq[:, :], ALU.add)
        P.mm(PSB[0][:, 0:MEM], ones32, acc[:, :])
        rms_scale(PSB[0][:, 0:MEM], std[:, :], rstd[:, :], D, MEM)
        memnT0 = P.sb("memnT0", [128, KC, MEM], BF16)
        for k in range(KC):
            P.stt(memnT0[:, k, :], mt[:, k, :], C_("mem_norm", k), rstd[:, :], ALU.mult, ALU.mult)
        P.dma(V(memn_d.rearrange("(k p) m -> p k m", p=128), None), memnT0[:, :, :])

    def phase_inproj(l):
        xsrc = x_in if l == 0 else xs
        with P.phase():
            hT = P.sb("hT", [128, KC, TT], BF16)
            xin = [P.sb(f"xin{i}", [128, TT], F32) for i in range(3)]
            sq = P.sb("sq", [128, TT], F32)
            acc = P.sb("acc", [128, TT], F32)
            std = P.sb("std", [128, TT], F32)
            rstd = P.sb("rstd", [128, TT], F32)
            zst = [P.sb(f"zst{i}", [128, TT], F32) for i in range(3)]
            vst = [P.sb(f"vst{i}", [128, 512], BF16) for i in range(2)]
            wsl = [P.sb(f"wA{i}", [128, KC * 512], BF16) for i in range(2)]
            kmT = None
            for j in range(NT):
                tsl = slice(j * TT, (j + 1) * TT)
                for k in range(KC):
                    xi = xin[k % 3]
                    P.dma(xi[:, :], V(xsrc[k * 128:(k + 1) * 128, tsl], None))
                    if k == 0:
                        P.act(acc[:, :], xi[:, :], AF.Square)
                    else:
                        P.act(sq[:, :], xi[:, :], AF.Square)
                        P.tt(acc[:, :], acc[:, :], sq[:, :], ALU.add)
                P.mm(PSB[4][:, 0:TT], ones32, acc[:, :])
                rms_scale(PSB[4][:, 0:TT], std[:, :], rstd[:, :], D, TT)
                for k in range(KC):
                    xi = xin[k % 3]
                    P.dma(xi[:, :], V(xsrc[k * 128:(k + 1) * 128, tsl], None))
                    P.stt(hT[:, k, :], xi[:, :], C_("norm_mix", l * KC + k), rstd[:, :], ALU.mult, ALU.mult)
                groups = [(0, 1024, 0), (2048, 1024, 1024), (3088, 1024, 2048), (5136, 1024, 3072),
                          (6168, 1024, 4096), (7192, 256, 5120), (3072, 16, 5376), (6160, 4, 5392),
                          (6164, 4, 5396)]
                ev = 0
                for (c0, ncols, z0) in groups:
                    for p0 in range(0, ncols, 512):
                        pw = min(512, ncols - p0)
                        wv = load_panel(wsl, l, "w_in", 0, KC, c0 + p0, pw)
                        for m0 in range(0, pw, 128):
                            M = min(128, pw - m0)
                            ps = PSB[ev % 4]
                            for k in range(KC):
                                P.mm(ps[0:M, 0:TT], V(wv.ap[:, k, m0:m0 + M], wv.b), hT[:, k, :],
                                     start=(k == 0), stop=(k == KC - 1))
                            st = zst[ev % 3]
                            if ev % 2 == 0:
                                P.cp(st[0:M, :], ps[0:M, 0:TT], eng="act")
                            else:
                                P.cp(st[0:M, :], ps[0:M, 0:TT], eng="dve")
                            P.dma(V(zf[z0 + p0 + m0: z0 + p0 + m0 + M, tsl], None), st[0:M, :])
                            ev += 1
                for (c0, v0) in ((1024, 0), (4112, 1024)):
                    for p0 in range(0, 1024, 512):
                        wv = load_panel(wsl, l, "w_in", 0, KC, c0 + p0, 512)
                        for tb in range(BPT):
                            ps = PSB[ev % 4]
                            for k in range(KC):
                                P.mm(ps[:, :], hT[:, k, tb * BLK:(tb + 1) * BLK], V(wv.ap[:, k, :], wv.b),
                                     start=(k == 0), stop=(k == KC - 1))
                            st = vst[ev % 2]
                            if ev % 2 == 0:
                                P.cp(st[:, :], ps[:, :], eng="act")
                            else:
                                P.cp(st[:, :], ps[:, :], eng="dve")
                            r0 = j * TT + tb * BLK
                            P.dma(V(vtok[r0:r0 + BLK, v0 + p0:v0 + p0 + 512], None), st[:, :])
                            ev += 1

    def phase_memkv(l, kmT, vm):
        with P.phase():
            wsl = [P.sb(f"wM{i}", [128, KC * 512], BF16) for i in range(2)]
            memnT = P.sb("memnT", [128, KC, MEM], BF16)
            P.dma(memnT[:, :, :], V(memn_d.rearrange("(k p) m -> p k m", p=128), None))
            ev = 0
            for p0 in range(0, MIXV, 512):
                wv = load_panel(wsl, l, "w_mk", 0, KC, p0, 512)
                for m0 in range(0, 512, 128):
                    ps = PSB[ev % 4]
                    for k in range(KC):
                        P.mm(ps[:, 0:MEM], V(wv.ap[:, k, m0:m0 + 128], wv.b), memnT[:, k, :],
                             start=(k == 0), stop=(k == KC - 1))
                    P.cp(kmT[:, (p0 + m0) // 128, :], ps[:, 0:MEM], eng="act" if ev % 2 else "dve")
                    ev += 1
            for p0 in range(0, MIXV, 512):
                wv = load_panel(wsl, l, "w_mv", 0, KC, p0, 512)
                for mc in range(MEM // 128):
                    ps = PSB[ev % 4]
                    for k in range(KC):
                        P.mm(ps[:, :], memnT[:, k, mc * 128:(mc + 1) * 128], V(wv.ap[:, k, :], wv.b),
                             start=(k == 0), stop=(k == KC - 1))
                    P.cp(vm[:, mc, p0:p0 + 512], ps[:, :], eng="act" if ev % 2 else "dve")
                    ev += 1

    def phase_memattn(l, kmT, vm):
        with P.phase():
            xq32 = [P.sb(f"xq32_{i}", [128, 2, TT], F32) for i in range(2)]
            xqb = [P.sb(f"xqb{i}", [128, 2, TT], BF16) for i in range(2)]
            pT = [P.sb(f"pT{i}", [128, 2, TT], BF16) for i in range(2)]
            rden = [P.sb(f"rden{i}", [128, TT], F32) for i in range(2)]
            ost = [P.sb(f"ost{i}", [128, TT], BF16) for i in range(3)]
            it = 0
            oi = 0
            for j in range(NT):
                tsl = slice(j * TT, (j + 1) * TT)
                for h in range(NH):
                    a = it % 2
                    it += 1
                    src = zf[4096 + h * 256:4096 + (h + 1) * 256, tsl].rearrange("(c p) t -> p c t", p=128)
                    P.dma(xq32[a][:, :, :], V(src, None))
                    P.cp(xqb[a][:, :, :], xq32[a][:, :, :])
                    for mc in range(2):
                        ps = PSB[mc]
                        for c in range(2):
                            P.mm(ps[:, 0:TT], kmT[:, h * 2 + c, mc * 128:(mc + 1) * 128], xqb[a][:, c, :],
                                 start=(c == 0), stop=(c == 1))
                        P.act(pT[a][:, mc, :], ps[:, 0:TT], AF.Exp, scale=DV ** -0.5)
                    for mc in range(2):
                        P.mm(PSB[2][:, 0:TT], onesb[:, :], pT[a][:, mc, :], start=(mc == 0), stop=(mc == 1))
                    P.recip(rden[a][:, :], PSB[2][:, 0:TT])
                    for c in range(2):
                        ps = PSB[3 + c]
                        for mc in range(2):
                            P.mm(ps[:, 0:TT], vm[:, mc, h * 256 + c * 128:h * 256 + (c + 1) * 128], pT[a][:, mc, :],
                                 start=(mc == 0), stop=(mc == 1))
                        o = ost[oi % 3]
                        oi += 1
                        P.tt(o[:, :], ps[:, 0:TT], rden[a][:, :], ALU.mult)
                        r0 = 2 * MIXV + h * 256 + c * 128
                        P.dma(V(obr[r0:r0 + 128, tsl], None), o[:, :])

    def phase_conv(l):
        with P.phase():
            hs = P.sb("hs", [128, 8, 8], F32)
            P.memset(hs[:, :, :], 0.0)
            src = zf[2048:3072, T - 3:T].rearrange("(c p) t -> p c t", p=128)
            P.dma(hs[:, :, 0:3], V(src, None))
            P.barrier()
            P.dma(V(halo_src.rearrange("(c p) w -> p c w", p=128), None), hs[:, :, :])
        collective_ag(halo_src, halo_all)
        with P.phase():
            ha = P.sb("ha", [128, NC, 8, 8], F32)
            hh = P.sb("hh", [128, 8, 8], F32)
            P.dma(ha[:, :, :, :], V(halo_all.rearrange("(r c p) w -> p r c w", p=128, c=8), None))
            P.memset(hh[:, :, :], 0.0)
            for r in range(NC):
                P.stt(hh[:, :, :], ha[:, r, :, :], C_("selprev", r), hh[:, :, :], ALU.mult, ALU.add)
            u = [P.sb(f"cu{i}", [128, T + 3], F32) for i in range(2)]
            y = [P.sb(f"cy{i}", [128, T], F32) for i in range(2)]
            for c in range(8):
                a = c % 2
                P.dma(u[a][:, 3:T + 3], V(zf[2048 + c * 128:2048 + (c + 1) * 128, :], None))
                P.cp(u[a][:, 0:3], hh[:, c, 0:3])
                P.ts(y[a][:, :], u[a][:, 0:T], C_("conv_w", (l * 4 + 0) * 8 + c), ALU.mult,
                     C_("conv_b", l * 8 + c), ALU.add)
                for jx in range(1, 4):
                    P.stt(y[a][:, :], u[a][:, jx:jx + T], C_("conv_w", (l * 4 + jx) * 8 + c), y[a][:, :],
                          ALU.mult, ALU.add)
                P.act(y[a][:, :], y[a][:, :], AF.Silu)
                P.dma(V(zc[c * 128:(c + 1) * 128, :], None), y[a][:, :])

    def mixer_pass(l, final, st):
        S, Sb, Cn, Cb, nbc, Dtot = st["S"], st["Sb"], st["Cn"], st["Cb"], st["nbc"], st["Dtot"]
        with P.phase():
            mi = P.sb("mi", [4, T], F32)
            mf = P.sb("mf", [4, T], F32)
            sp_ = P.sb("sp", [4, T], F32)
            cs = P.sb("cs", [4, T], F32)
            aa = P.sb("aa", [4, T], F32)
            pm = P.sb("pm", [4, T], F32)
            tmp4 = P.sb("tmp4", [4, T], F32)
            G = P.sb("G", [4, NB, 4, BLK], F32)
            AB = P.sb("AB", [4, NB, 2], F32)
            gl_ = P.sb("gl", [4, NB], F32)
            mnx = P.sb("mnx", [4, NB + 1], F32)
            g4m = P.sb("g4m", [4, 2 * T], F32)
            P.dma(g4m[:, :], V(g4m_in, None))
            wga = P.sb("wga", [RANK, 512], F32)
            P.dma(wga[:, :], V(wga_in[:, l * 512:(l + 1) * 512], None))
            P.dma(mi[:, :], V(zf[5392:5396, :], None))
            P.dma(mf[:, :], V(zf[5396:5400, :], None))
            nbf = g4[:, G4_B + 4 * l + 0:G4_B + 4 * l + 1]
            bi = g4[:, G4_B + 4 * l + 1:G4_B + 4 * l + 2]
            P.act(sp_[:, :], mf[:, :], AF.Exp, bias=nbf, scale=-1.0)
            P.act(sp_[:, :], sp_[:, :], AF.Ln, bias=1.0)
            P.scan(cs[:, :], g4m[:, 0:T], sp_[:, :], 0.0, ALU.mult, ALU.add)
            P.stt(aa[:, :], mi[:, :], bi, cs[:, :], ALU.add, ALU.add)
            P.scan(pm[:, :], g4m[:, T:2 * T], aa[:, :], NEG, ALU.add, ALU.max)
            csl = V(cs.t[:, BLK - 1::BLK], cs.b)
            pml = V(pm.t[:, BLK - 1::BLK], pm.b)
            P.tt(gl_[:, :], pml, csl, ALU.subtract)
            P.ts(tmp4[:, 0:NB], csl, -1.0, ALU.mult)
            P.cp(mnx[:, 0:1], st["m0"][:, 0:1])
            P.scan(mnx[:, 1:NB + 1], tmp4[:, 0:NB], gl_[:, :], st["m0"][:, 0:1], ALU.add, ALU.max)
            P.tt(tmp4[:, NB:2 * NB], tmp4[:, 0:NB], mnx[:, 1:NB + 1], ALU.subtract)
            P.act(V(AB.t[:, :, 1], AB.b), tmp4[:, NB:2 * NB], AF.Exp)
            P.tt(tmp4[:, NB:2 * NB], tmp4[:, NB:2 * NB], mnx[:, 0:NB], ALU.add)
            P.act(V(AB.t[:, :, 0], AB.b), tmp4[:, NB:2 * NB], AF.Exp)
            lnk = P.sb("lnk", [4, 1], F32)
            P.memset(lnk[:, :], -0.5 * math.log(DK))
            for b in range(NB):
                bs = slice(b * BLK, (b + 1) * BLK)
                P.act(G[:, b, 0, :], aa[:, bs], AF.Exp, bias=lnk[:, 0:1])
                P.ts(tmp4[:, bs], pm[:, bs], mnx[:, b:b + 1], ALU.max)
                P.act(G[:, b, 1, :], tmp4[:, bs], AF.Exp, scale=-1.0)
                P.tt(sp_[:, bs], cs[:, bs], tmp4[:, bs], ALU.subtract)
                P.act(G[:, b, 3, :], sp_[:, bs], AF.Exp)
                P.ts(tmp4[:, bs], tmp4[:, bs], mnx[:, b:b + 1], ALU.subtract)
                P.act(G[:, b, 2, :], tmp4[:, bs], AF.Exp, scale=-1.0)
            if not final:
                P.cp(st["scal"][:, 0:1], mnx[:, NB:NB + 1])
                P.add("dve", lambda e: e.tensor_reduce(st["scal"].t[:, 1:2], csl.ap, mybir.AxisListType.X, ALU.add),
                      r=[cs.b], w=[st["scal"].b])
                P.ts(st["scal"][:, 1:2], st["scal"][:, 1:2], -1.0, ALU.mult)

            qk = [P.sb(f"qk{i}", [128, 16, BLK], F32) for i in range(2)]
            gab = [P.sb(f"ga{i}", [RANK, BLK], F32) for i in range(2)]
            vb = [P.sb(f"vb{i}", [128, 2048], BF16) for i in range(2)]
            gat = [P.sb(f"gat{i}", [128, 16, BLK], F32) for i in range(2)] if final else None
            la = [P.sb(f"la{i}", [128, BLK], F32) for i in range(2)]
            cum = [P.sb(f"cum{i}", [128, BLK], F32) for i in range(2)]
            eb = [P.sb(f"eb{i}", [128, BLK], F32) for i in range(2)]
            enb = [P.sb(f"enb{i}", [128, BLK], F32) for i in range(2)]
            bcs = [P.sb(f"bcs{i}", [128, 512], F32) for i in range(2)]
            abs_ = [P.sb(f"abs{i}", [128, 2], F32) for i in range(2)]
            qd = [P.sb(f"qd{i}", [128, BLK], BF16) for i in range(2)]
            qh = [P.sb(f"qh{i}", [128, BLK], BF16) for i in range(2)]
            kd = [P.sb(f"kd{i}", [128, BLK], BF16) for i in range(2)]
            kdt = [P.sb(f"kdt{i}", [128, BLK], BF16) for i in range(2)]
            att = [P.sb(f"att{i}", [128, BLK], BF16) for i in range(2)]
            tmpS = [P.sb(f"tmpS{i}", [128, DV + 1], F32) for i in range(2)]
            o32 = [P.sb(f"o32{i}", [128, 2, BLK], F32) for i in range(2)]
            osq = [P.sb(f"osq{i}", [128, 2, BLK], BF16) for i in range(2)]
            dn = [P.sb(f"dn{i}", [128, BLK], F32) for i in range(2)]
            sg = [P.sb(f"sg{i}", [128, 2, BLK], F32) for i in range(2)]
            oo = [P.sb(f"oo{i}", [128, 2, BLK], BF16) for i in range(3)]
            onesc = P.sb("onesc", [128, 1], BF16)
            P.memset(onesc[:, :], 1.0)
            oi = 0
            for b in range(NB):
                a = b % 2
                bs = slice(b * BLK, (b + 1) * BLK)
                P.dma(qk[a][:, 0:8, :], V(zf[0:1024, bs].rearrange("(c p) t -> p c t", p=128), None))
                P.dma(qk[a][:, 8:16, :], V(zc[0:1024, bs].rearrange("(c p) t -> p c t", p=128), None))
                P.dma(gab[a][:, :], V(zf[5376:5392, bs], None))
                P.dma(vb[a][:, :], V(vtok[bs, :], None))
                if final:
                    P.dma(gat[a][:, 0:8, :], V(zf[1024:2048, bs].rearrange("(c p) t -> p c t", p=128), None))
                    P.dma(gat[a][:, 8:16, :], V(zf[3072:4096, bs].rearrange("(c p) t -> p c t", p=128), None))
                for h in range(NH):
                    p = h % 2
                    X, Y, Z = PSB[3 * p], PSB[3 * p + 1], PSB[3 * p + 2]
                    P.mm(X[:, 0:BLK], wga[:, h * 128:(h + 1) * 128], gab[a][:, :])
                    P.act(la[p][:, :], X[:, 0:BLK], AF.Exp, bias=C_("nb_gla_a", l * 4 + h), scale=-1.0)
                    P.act(la[p][:, :], la[p][:, :], AF.Ln, bias=1.0)
                    P.scan(cum[p][:, :], ones32, la[p][:, :], 0.0, ALU.mult, ALU.add)
                    P.act(eb[p][:, :], cum[p][:, :], AF.Exp, scale=-1.0 / 16.0)
                    P.act(enb[p][:, :], cum[p][:, :], AF.Exp, scale=1.0 / 16.0)
                    P.tt(kd[p][:, :], qk[a][:, 4 + h, :], enb[p][:, :], ALU.mult)
                    P.tr(PST[:, 0:BLK], kd[p][:, :], identb[:, :])
                    P.cp(kdt[p][:, :], PST[:, 0:BLK], eng="act")
                    if final:
                        P.stt(qd[p][:, :], qk[a][:, h, :], DK ** -0.5, eb[p][:, :], ALU.mult, ALU.mult)
                        P.mm(Y[:, 0:BLK], kd[p][:, :], qd[p][:, :])
                        P.tt(att[p][:, :], Y[:, 0:BLK], maskT, ALU.mult)
                        for c in range(2):
                            P.mm(Z[:, c * BLK:(c + 1) * BLK], vb[a][:, h * DV + c * 128:h * DV + (c + 1) * 128],
                                 att[p][:, :], start=True, stop=False)
                            P.mm(Z[:, c * BLK:(c + 1) * BLK], Sb[h][:, c * 128:(c + 1) * 128], qd[p][:, :],
                                 start=False, stop=True)
                        P.cp(V(o32[p].t[:, :, :], o32[p].b),
                             V(Z.t[:, 0:2 * BLK].rearrange("p (c t) -> p c t", c=2), Z.b), eng="act")
                        P.tt(osq[p][:, :, :], o32[p][:, :, :], o32[p][:, :, :], ALU.mult)
                        for c in range(2):
                            P.mm(Z[:, 2 * BLK:3 * BLK], onesb[:, :], osq[p][:, c, :], start=(c == 0), stop=(c == 1))
                        P.act(dn[p][:, :], Z[:, 2 * BLK:3 * BLK], AF.Sqrt, bias=epsc[:, 0:1], scale=1.0 / DV)
                        P.recip(dn[p][:, :], dn[p][:, :])
                        P.act(sg[p][:, :, :], gat[a][:, 2 * h:2 * h + 2, :], AF.Silu)
                        o = oo[oi % 3]
                        oi += 1
                        for c in range(2):
                            P.stt(o32[p][:, c, :], o32[p][:, c, :], C_("gla_norm", l * 8 + 2 * h + c), dn[p][:, :],
                                  ALU.mult, ALU.mult)
                            P.tt(o[:, c, :], o32[p][:, c, :], sg[p][:, c, :], ALU.mult)
                        P.dma(V(obr[h * DV:(h + 1) * DV, bs].rearrange("(c p) t -> p c t", p=128), None), o[:, :, :])
                    P.mm(Y[:, 128:128 + DV], kdt[p][:, :], vb[a][:, h * DV:(h + 1) * DV])
                    P.tt(tmpS[p][:, 0:DV], S[h][:, :], Y[:, 128:128 + DV], ALU.add)
                    P.ts(S[h][:, :], tmpS[p][:, 0:DV], eb[p][:, BLK - 1:BLK], ALU.mult)
                    if final:
                        P.cp(Sb[h][:, :], S[h][:, :], eng="act")
                    else:
                        P.tt(Dtot[:, h:h + 1], Dtot[:, h:h + 1], eb[p][:, BLK - 1:BLK], ALU.mult)
                for h in range(NH):
                    p = h % 2
                    X, Y, Z = PSB[3 * p], PSB[3 * p + 1], PSB[3 * p + 2]
                    eh = g4[:, h * 128:(h + 1) * 128]
                    P.mm(X[:, :], eh, V(G.t[:, b, :, :].rearrange("p r t -> p (r t)"), G.b))
                    P.cp(bcs[p][:, :], X[:, :], eng="act")
                    P.mm(PSB[6][:, 0:2], eh, AB[:, b, :])
                    P.cp(abs_[p][:, :], PSB[6][:, 0:2], eng="act")
                    P.tt(kd[p][:, :], qk[a][:, 12 + h, :], bcs[p][:, 0:BLK], ALU.mult)
                    P.tr(PST[:, 0:BLK], kd[p][:, :], identb[:, :])
                    P.cp(kdt[p][:, :], PST[:, 0:BLK], eng="act")
                    if final:
                        P.tt(qd[p][:, :], qk[a][:, 8 + h, :], bcs[p][:, BLK:2 * BLK], ALU.mult)
                        P.tt(qh[p][:, :], qk[a][:, 8 + h, :], bcs[p][:, 2 * BLK:3 * BLK], ALU.mult)
                        P.mm(Y[:, 0:BLK], kd[p][:, :], qd[p][:, :])
                        P.tt(att[p][:, :], Y[:, 0:BLK], maskT, ALU.mult)
                        for c in range(2):
                            P.mm(Z[:, c * BLK:(c + 1) * BLK],
                                 vb[a][:, 1024 + h * DV + c * 128:1024 + h * DV + (c + 1) * 128],
                                 att[p][:, :], start=True, stop=False)
                            P.mm(Z[:, c * BLK:(c + 1) * BLK], Cb[h][:, c * 128:(c + 1) * 128], qh[p][:, :],
                                 start=False, stop=True)
                        P.mm(Z[:, 2 * BLK:3 * BLK], onesb[:, :], att[p][:, :], start=True, stop=False)
                        P.mm(Z[:, 2 * BLK:3 * BLK], nbc[h][:, :], qh[p][:, :], start=False, stop=True)
                        P.act(dn[p][:, :], Z[:, 2 * BLK:3 * BLK], AF.Abs)
                        P.tt(dn[p][:, :], dn[p][:, :], bcs[p][:, 3 * BLK:4 * BLK], ALU.max)
                        P.recip(dn[p][:, :], dn[p][:, :])
                        P.act(sg[p][:, :, :], gat[a][:, 8 + 2 * h:8 + 2 * h + 2, :], AF.Sigmoid)
                        for c in range(2):
                            P.tt(o32[p][:, c, :], Z[:, c * BLK:(c + 1) * BLK], dn[p][:, :], ALU.mult)
                        P.tt(o32[p][:, :, :], o32[p][:, :, :], sg[p][:, :, :], ALU.mult)
                        P.tt(osq[p][:, :, :], o32[p][:, :, :], o32[p][:, :, :], ALU.mult)
                        for c in range(2):
                            P.mm(Z[:, 3 * BLK:4 * BLK], onesb[:, :], osq[p][:, c, :], start=(c == 0), stop=(c == 1))
                        P.act(dn[p][:, :], Z[:, 3 * BLK:4 * BLK], AF.Sqrt, bias=epsc[:, 0:1], scale=1.0 / DV)
                        P.recip(dn[p][:, :], dn[p][:, :])
                        o = oo[oi % 3]
                        oi += 1
                        for c in range(2):
                            P.stt(o[:, c, :], o32[p][:, c, :], C_("ml_norm", l * 8 + 2 * h + c), dn[p][:, :],
                                  ALU.mult, ALU.mult)
                        P.dma(V(obr[MIXV + h * DV:MIXV + (h + 1) * DV, bs].rearrange("(c p) t -> p c t", p=128), None),
                              o[:, :, :])
                    P.mm(Y[:, 128:128 + DV], kdt[p][:, :], vb[a][:, 1024 + h * DV:1024 + (h + 1) * DV])
                    P.mm(Y[:, 128 + DV:128 + DV + 1], kdt[p][:, :], onesc[:, :])
                    P.ts(tmpS[p][:, :], Y[:, 128:128 + DV + 1], abs_[p][:, 1:2], ALU.mult)
                    P.stt(Cn[h][:, :], Cn[h][:, :], abs_[p][:, 0:1], tmpS[p][:, :], ALU.mult, ALU.add)
                    if final:
                        P.cp(Cb[h][:, :], Cn[h][:, 0:DV], eng="act")
                        P.ts(nbc[h][:, :], ones32, Cn[h][:, DV:DV + 1], ALU.mult)

    def phase_mixers(l):
        with P.phase():
            st = {
                "S": [P.sb(f"S{h}", [128, DV], F32) for h in range(NH)],
                "Sb": [P.sb(f"Sb{h}", [128, DV], BF16) for h in range(NH)],
                "Cn": [P.sb(f"Cn{h}", [128, DV + 1], F32) for h in range(NH)],
                "Cb": [P.sb(f"Cb{h}", [128, DV], BF16) for h in range(NH)],
                "nbc": [P.sb(f"nbc{h}", [128, 128], BF16) for h in range(NH)],
                "Dtot": P.sb("Dtot", [128, NH], F32),
                "m0": P.sb("m0", [4, 1], F32),
                "scal": P.sb("scal", [4, 2], F32),
            }
            for h in range(NH):
                P.memset(st["S"][h][:, :], 0.0)
                P.memset(st["Cn"][h][:, :], 0.0)
            P.memset(st["Dtot"][:, :], 1.0)
            P.memset(st["m0"][:, :], NEG)
            mixer_pass(l, False, st)
            pub = [P.sb(f"pub{i}", [128, XW], F32) for i in range(3)]
            pi_ = 0
            for h in range(NH):
                pb_ = pub[pi_ % 3]
                pi_ += 1
                P.memset(pb_[:, :], 0.0)
                P.cp(pb_[:, 0:DV], st["S"][h][:, :])
                P.cp(pb_[:, DV:DV + 1], st["Dtot"][:, h:h + 1])
                P.dma(V(xch_src[h * 128:(h + 1) * 128, :], None), pb_[:, :])
                pb_ = pub[pi_ % 3]
                pi_ += 1
                P.memset(pb_[:, :], 0.0)
                P.cp(pb_[:, 0:DV + 1], st["Cn"][h][:, :])
                P.dma(V(xch_src[512 + h * 128:512 + (h + 1) * 128, :], None), pb_[:, :])
            pb_ = pub[pi_ % 3]
            P.memset(pb_[:, :], 0.0)
            P.cp(pb_[0:4, 0:2], st["scal"][:, :])
            P.dma(V(xch_src[1024:1152, :], None), pb_[:, :])
            collective_ag(xch_src, xch_all)
            with P.phase():
                xa = xch_all.rearrange("(r q) w -> r q w", q=XR)
                sc = P.sb("sc", [4, NC, 2], F32)
                P.dma(sc[:, :, :], V(xa[:, 1024:1028, 0:2].rearrange("r h c -> h r c"), None))
                sel4 = g4[:, G4_SEL:G4_SEL + NC]
                fte = P.sb("fte", [4, NC], F32)
                mre = P.sb("mre", [4, NC], F32)
                t4 = P.sb("t4", [4, NC], F32)
                mseq = P.sb("mseq", [4, NC + 1], F32)
                ab = P.sb("ab", [4, NC, 2], F32)
                P.tt(fte[:, :], V(sc.t[:, :, 1], sc.b), sel4, ALU.mult)
                P.tt(mre[:, :], V(sc.t[:, :, 0], sc.b), sel4, ALU.mult)
                P.ts(t4[:, :], sel4, -1.0, ALU.add, 1e30, ALU.mult)
                P.tt(mre[:, :], mre[:, :], t4[:, :], ALU.add)
                P.memset(mseq[:, 0:1], NEG)
                P.scan(mseq[:, 1:NC + 1], fte[:, :], mre[:, :], NEG, ALU.add, ALU.max)
                P.tt(t4[:, :], fte[:, :], mseq[:, 0:NC], ALU.add)
                P.tt(t4[:, :], t4[:, :], mseq[:, 1:NC + 1], ALU.subtract)
                P.act(V(ab.t[:, :, 0], ab.b), t4[:, :], AF.Exp)
                P.tt(t4[:, :], mre[:, :], mseq[:, 1:NC + 1], ALU.subtract)
                P.act(V(ab.t[:, :, 1], ab.b), t4[:, :], AF.Exp)
                P.tt(V(ab.t[:, :, 1], ab.b), V(ab.t[:, :, 1], ab.b), sel4, ALU.mult)
                abb = [P.sb(f"abb{h}", [128, NC * 2], F32) for h in range(NH)]
                for h in range(NH):
                    P.mm(PSB[6][:, 0:NC * 2], g4[:, h * 128:(h + 1) * 128],
                         V(ab.t[:, :, :].rearrange("p r c -> p (r c)"), ab.b))
                    P.cp(abb[h][:, :], PSB[6][:, 0:NC * 2], eng="act")
                P.cp(st["m0"][:, 0:1], mseq[:, NC:NC + 1])
                for h in range(NH):
                    P.memset(st["S"][h][:, :], 0.0)
                    P.memset(st["Cn"][h][:, :], 0.0)
                xr = [P.sb(f"xr{i}", [128, 8, XW], F32) for i in range(2)]
                de = P.sb("de", [128, 1], F32)
                tS = P.sb("tS", [128, DV + 1], F32)
                for r in range(NC):
                    a = r % 2
                    P.dma(xr[a][:, :, :], V(xa[r, 0:1024, :].rearrange("(g p) w -> p g w", p=128), None))
                    for h in range(NH):
                        P.ts(de[:, :], xr[a][:, h, DV:DV + 1], -1.0, ALU.add, C_("sel", r), ALU.mult)
                        P.ts(de[:, :], de[:, :], 1.0, ALU.add)
                        P.ts(tS[:, 0:DV], xr[a][:, h, 0:DV], C_("sel", r), ALU.mult)
                        P.stt(st["S"][h][:, :], st["S"][h][:, :], de[:, 0:1], tS[:, 0:DV], ALU.mult, ALU.add)
                        P.ts(tS[:, :], xr[a][:, 4 + h, 0:DV + 1], abb[h][:, 2 * r + 1:2 * r + 2], ALU.mult)
                        P.stt(st["Cn"][h][:, :], st["Cn"][h][:, :], abb[h][:, 2 * r:2 * r + 1], tS[:, :],
                              ALU.mult, ALU.add)
                for h in range(NH):
                    P.cp(st["Sb"][h][:, :], st["S"][h][:, :], eng="act")
                    P.cp(st["Cb"][h][:, :], st["Cn"][h][:, 0:DV], eng="act")
                    P.ts(st["nbc"][h][:, :], ones32, st["Cn"][h][:, DV:DV + 1], ALU.mult)
            mixer_pass(l, True, st)

    def phase_merge(l):
        xsrc = x_in if l == 0 else xs
        if l + 1 < L:
            prep_weights(l + 1)
        with P.phase():
            aT = P.sb("yT", [128, KC, TT], BF16)
            ob = P.sb("ob", [128, 24, TT], BF16)
            gl32 = P.sb("gl32", [128, 2, TT], F32)
            glb = P.sb("glb", [128, 2, TT], BF16)
            gt = [P.sb(f"gt{i}", [128, TT], F32) for i in range(2)]
            y32 = [P.sb(f"y32{i}", [128, TT], F32) for i in range(2)]
            t32 = [P.sb(f"t32{i}", [128, TT], F32) for i in range(2)]
            xc = [P.sb(f"xc{i}", [128, TT], F32) for i in range(3)]
            wsl = [P.sb(f"wG{i}", [128, 32 * 256], BF16) for i in range(2)]
            ev = 0
            xi = 0
            for j in range(NT):
                tsl = slice(j * TT, (j + 1) * TT)
                P.dma(ob[:, :, :], V(obr[:, tsl].rearrange("(k p) t -> p k t", p=128), None))
                P.dma(gl32[:, :, :], V(zf[5120:5376, tsl].rearrange("(k p) t -> p k t", p=128), None))
                P.cp(glb[:, :, :], gl32[:, :, :], eng="act")
                for p0 in range(0, D, 256):
                    wv = load_panel(wsl, l, "w_bg", 0, 30, p0, 256)
                    for m0 in range(0, 256, 128):
                        oc = (p0 + m0) // 128
                        yy = y32[oc % 2]
                        for br in range(3):
                            pg = PSB[ev % 6]
                            ev += 1
                            for k in range(2):
                                P.mm(pg[:, 0:TT], V(wv.ap[:, 24 + br * 2 + k, m0:m0 + 128], wv.b), glb[:, k, :],
                                     start=(k == 0), stop=(k == 1))
                            g_ = gt[ev % 2]
                            P.act(g_[:, :], pg[:, 0:TT], AF.Sigmoid, bias=C_("b_gate", (l * 3 + br) * KC + oc))
                            pb = PSB[ev % 6]
                            ev += 1
                            for k in range(8):
                                P.mm(pb[:, 0:TT], V(wv.ap[:, br * 8 + k, m0:m0 + 128], wv.b), ob[:, br * 8 + k, :],
                                     start=(k == 0), stop=(k == 7))
                            if br == 0:
                                P.tt(yy[:, :], g_[:, :], pb[:, 0:TT], ALU.mult)
                            elif br == 1:
                                tt_ = t32[oc % 2]
                                P.tt(tt_[:, :], g_[:, :], pb[:, 0:TT], ALU.mult)
                                P.tt(yy[:, :], yy[:, :], tt_[:, :], ALU.add)
                            else:
                                tt_ = t32[oc % 2]
                                P.tt(tt_[:, :], g_[:, :], pb[:, 0:TT], ALU.mult)
                                P.tt(aT[:, oc, :], yy[:, :], tt_[:, :], ALU.add)
                for p0 in range(0, D, 256):
                    wv = load_panel(wsl, l, "w_out", 0, KC, p0, 256)
                    for m0 in range(0, 256, 128):
                        oc = (p0 + m0) // 128
                        x_ = xc[xi % 3]
                        xi += 1
                        P.dma(x_[:, :], V(xsrc[oc * 128:(oc + 1) * 128, tsl], None))
                        ps = PSB[ev % 6]
                        ev += 1
                        for k in range(KC):
                            P.mm(ps[:, 0:TT], V(wv.ap[:, k, m0:m0 + 128], wv.b), aT[:, k, :],
                                 start=(k == 0), stop=(k == KC - 1))
                        P.tt(x_[:, :], x_[:, :], ps[:, 0:TT], ALU.add)
                        P.dma(V(xs[oc * 128:(oc + 1) * 128, tsl], None), x_[:, :])

    def phase_ffn(l):
        last = (l == L - 1)
        with P.phase():
            xt = P.sb("xt", [128, KC, TT], F32)
            aT = P.sb("h2T", [128, KC, TT], BF16)
            t32 = [P.sb(f"r32{i}", [128, TT], F32) for i in range(2)]
            aseg = P.sb("aseg", [128, SEG, TT], BF16)
            wsl = [P.sb(f"wF{i}", [128, 32 * 256], BF16) for i in range(2)]
            sq = P.sb("fsq", [128, TT], F32)
            acc = P.sb("facc", [128, TT], F32)
            std = P.sb("fstd", [128, TT], F32)
            rstd = P.sb("frstd", [128, TT], F32)
            ev = 0
            for j in range(NT):
                tsl = slice(j * TT, (j + 1) * TT)
                P.dma(xt[:, :, :], V(xs[:, tsl].rearrange("(k p) t -> p k t", p=128), None))
                for k in range(KC):
                    if k == 0:
                        P.act(acc[:, :], xt[:, k, :], AF.Square)
                    else:
                        P.act(sq[:, :], xt[:, k, :], AF.Square)
                        P.tt(acc[:, :], acc[:, :], sq[:, :], ALU.add)
                P.mm(PSB[6][:, 0:TT], ones32, acc[:, :])
                rms_scale(PSB[6][:, 0:TT], std[:, :], rstd[:, :], D, TT)
                for k in range(KC):
                    P.stt(aT[:, k, :], xt[:, k, :], C_("norm_ffn", l * KC + k), rstd[:, :], ALU.mult, ALU.mult)
                for sgi in range(NSEG):
                    for p0 in range(0, SEG * 128, 256):
                        wv = load_panel(wsl, l, "w_ff1", 0, KC, sgi * SEG * 128 + p0, 256)
                        for m0 in range(0, 256, 128):
                            ci = (p0 + m0) // 128
                            ps = PSB[ev % 6]
                            ev += 1
                            for k in range(KC):
                                P.mm(ps[:, 0:TT], V(wv.ap[:, k, m0:m0 + 128], wv.b), aT[:, k, :],
                                     start=(k == 0), stop=(k == KC - 1))
                            r_ = t32[ci % 2]
                            P.act(r_[:, :], ps[:, 0:TT], AF.Relu)
                            P.tt(aseg[:, ci, :], r_[:, :], r_[:, :], ALU.mult)
                    for p0 in range(0, D, 512):
                        pw = min(512, D - p0)
                        wv = load_panel(wsl, l, "w_ff2", sgi * SEG * 128, SEG, p0, pw)
                        for m0 in range(0, pw, 128):
                            oc = (p0 + m0) // 128
                            ps = PSB[ev % 6]
                            ev += 1
                            for k in range(SEG):
                                P.mm(ps[:, 0:TT], V(wv.ap[:, k, m0:m0 + 128], wv.b), aseg[:, k, :],
                                     start=(k == 0), stop=(k == SEG - 1))
                            P.tt(xt[:, oc, :], xt[:, oc, :], ps[:, 0:TT], ALU.add)
                if not last:
                    P.dma(V(xs[:, tsl].rearrange("(k p) t -> p k t", p=128), None), xt[:, :, :])
                else:
                    for k in range(KC):
                        if k == 0:
                            P.act(acc[:, :], xt[:, k, :], AF.Square)
                        else:
                            P.act(sq[:, :], xt[:, k, :], AF.Square)
                            P.tt(acc[:, :], acc[:, :], sq[:, :], ALU.add)
                    P.mm(PSB[6][:, 0:TT], ones32, acc[:, :])
                    rms_scale(PSB[6][:, 0:TT], std[:, :], rstd[:, :], D, TT)
                    for k in range(KC):
                        P.stt(xt[:, k, :], xt[:, k, :], C_("final_norm", k), rstd[:, :], ALU.mult, ALU.mult)
                    P.dma(V(outT[:, tsl].rearrange("(k p) t -> p k t", p=128), None), xt[:, :, :])

    kmT = P.sb("kmT", [128, 8, MEM], BF16)
    vm = P.sb("vm", [128, 2, MIXV], BF16)
    for l in range(L):
        phase_inproj(l)
        phase_memkv(l, kmT, vm)
        phase_memattn(l, kmT, vm)
        phase_conv(l)
        phase_mixers(l)
        phase_merge(l)
        phase_ffn(l)
    P.barrier()
    for e in ENGS:
        P.add(e, None)

    P.finalize(eng_sems)
    with nc.Block() as block:
        @block.tensor
        def _(e):
            P.emit("pe", e, eng_sems)

        @block.scalar
        def _(e):
            P.emit("act", e, eng_sems)

        @block.vector
        def _(e):
            P.emit("dve", e, eng_sems)

        @block.gpsimd
        def _(e):
            P.emit("pool", e, eng_sems)

        @block.sync
        def _(e):
            P.emit("sp", e, eng_sems)
    P.top.close()
    return nc


def _cols(vec):
    v = np.asarray(vec, np.float32).reshape(-1, 128)
    return np.ascontiguousarray(v.T)


def make_in_maps(inp, NC, T, D, DFF, L):
    KC = D // 128
    f32 = np.float32
    x = np.asarray(inp["x"], f32)[0]
    mem = np.asarray(inp["mem"], f32)[0]
    memT = np.ascontiguousarray(mem.T)
    parts = []
    parts.append(np.concatenate([_cols(inp["norm_mix"][l]) for l in range(L)], 1))
    parts.append(np.concatenate([_cols(inp["norm_ffn"][l]) for l in range(L)], 1))
    parts.append(_cols(inp["final_norm"]))
    parts.append(_cols(inp["mem_norm"]))
    parts.append(np.concatenate([_cols(inp["b_gate"][l][b]) for l in range(L) for b in range(3)], 1))
    parts.append(np.concatenate([_cols(inp["gla_norm"][l]) for l in range(L)], 1))
    parts.append(np.concatenate([_cols(inp["ml_norm"][l]) for l in range(L)], 1))
    parts.append(np.concatenate([_cols(-np.asarray(inp["b_gla_a"][l], f32)) for l in range(L)], 1))
    parts.append(np.concatenate([_cols(inp["conv_w"][l][j]) for l in range(L) for j in range(4)], 1))
    parts.append(np.concatenate([_cols(inp["conv_b"][l]) for l in range(L)], 1))
    cols_common = np.concatenate(parts, 1).astype(f32)
    consts = np.zeros((128, 384), f32)
    consts[:, 0:128] = np.triu(np.ones((128, 128), f32))
    consts[:, 128:256] = np.eye(128, dtype=f32)
    consts[:, 256:384] = 1.0
    g4c = np.zeros((4, 4 * 128 + 4 * L + 8), f32)
    for h in range(4):
        g4c[h, h * 128:(h + 1) * 128] = 1.0
    g4m = np.zeros((4, 2 * T), f32)
    rst = np.ones(T, f32)
    rst[0::BLK] = 0.0
    g4m[:, 0:T] = rst[None]
    rneg = np.zeros(T, f32)
    rneg[0::BLK] = NEG
    g4m[:, T:2 * T] = rneg[None]
    for l in range(L):
        g4c[:, 512 + 4 * l + 0] = -np.asarray(inp["b_ml_f"][l], f32)
        g4c[:, 512 + 4 * l + 1] = np.asarray(inp["b_ml_i"][l], f32)
    wga = np.concatenate([np.asarray(inp["w_gla_a"][l], f32) for l in range(L)], 1)
    wmats = {}
    for l in range(L):
        wmats[(l, "w_in")] = np.asarray(inp["w_in"][l], f32)
        wmats[(l, "w_mk")] = np.asarray(inp["w_mem_k"][l], f32)
        wmats[(l, "w_mv")] = np.asarray(inp["w_mem_v"][l], f32)
        wmats[(l, "w_bg")] = np.concatenate([np.asarray(inp["w_branch"][l], f32).reshape(3 * MIXV, D),
                                             np.asarray(inp["w_gate"][l], f32).reshape(3 * 256, D)], 0)
        wmats[(l, "w_out")] = np.asarray(inp["w_out"][l], f32)
        wmats[(l, "w_ff1")] = np.asarray(inp["w_ff1"][l], f32)
        wmats[(l, "w_ff2")] = np.asarray(inp["w_ff2"][l], f32)
    maps = []
    for c in range(NC):
        m = {}
        m["xT"] = np.ascontiguousarray(x[c * T:(c + 1) * T].T)
        m["memT"] = memT
        sel = np.zeros((128, 8), f32)
        sel[:, :c] = 1.0
        selp = np.zeros((128, 8), f32)
        if c > 0:
            selp[:, c - 1] = 1.0
        m["cols"] = np.ascontiguousarray(np.concatenate([cols_common, sel, selp], 1))
        m["consts"] = consts
        g = g4c.copy()
        g[:, 512 + 4 * L:512 + 4 * L + 8] = sel[:4, :]
        m["g4"] = g
        m["g4m"] = g4m
        m["wga"] = np.ascontiguousarray(wga)
        for (l, wn), w in wmats.items():
            R = w.shape[0] // NC
            m[f"{wn}_{l}"] = np.ascontiguousarray(w[c * R:(c + 1) * R])
        maps.append(m)
    return maps


_NC_CACHE = {}


def run(inp, NC, T, D, DFF, L, debug=False):
    key = (NC, T, D, DFF, L, debug)
    if key not in _NC_CACHE:
        _NC_CACHE[key] = build_program(NC, T, D, DFF, L, debug=debug)
    nc = _NC_CACHE[key]
    maps = make_in_maps(inp, NC, T, D, DFF, L)
    res = run_bass_kernel_spmd(nc, maps, core_ids=list(range(NC)))
    out = np.concatenate([np.asarray(r["outT"]).T for r in res.results], 0)[None]
    return out.astype(np.float32), res


def kernel(**inputs):
    out, _ = run(inputs, 8, 2048, 4096, 16384, 4)
    return out
```

```python
import math
from contextlib import ExitStack
import numpy as np
import concourse.bass as bass
import concourse.mybir as mybir
from concourse.bass_utils import run_bass_kernel_spmd

F32 = mybir.dt.float32
BF16 = mybir.dt.bfloat16
AF = mybir.ActivationFunctionType
ALU = mybir.AluOpType

NH = 4
DK = 128
DV = 256
MIXV = 1024
RANK = 16
MEM = 256
IN_W = 7448
EPS = 1e-6
BLK = 128
NEG = -1e30

ENGS = ("pe", "act", "dve", "pool", "sp")


class Buf:
    __slots__ = ("name", "last_w", "readers", "sem", "dcount", "last_dma")

    def __init__(self, name):
        self.name = name
        self.last_w = None
        self.readers = []
        self.sem = None
        self.dcount = 0
        self.last_dma = None


class V:
    __slots__ = ("ap", "b")

    def __init__(self, ap, b):
        self.ap = ap
        self.b = b


class Tl:
    def __init__(self, t, b):
        self.t = t
        self.b = b

    def __getitem__(self, idx):
        return V(self.t[idx], self.b)


class Op:
    __slots__ = ("eng", "fn", "deps", "sig", "sig_idx", "waits", "dma_buf", "dma_val", "inc", "extra", "gidx", "force")


class Prog:
    def __init__(self, nc, same_engine_sync=True):
        self.nc = nc
        self.ops = {e: [] for e in ENGS}
        self.last = {e: None for e in ENGS}
        self.pending = {e: None for e in ENGS}
        self.n = 0
        self.scopes = []
        self.sem_free = []
        self.sem_all = []
        self.live_dma_bufs = []
        self.top = ExitStack()
        self.ses = same_engine_sync
        self.uid = 0
        self.pe_fence = None

    def _stack(self):
        return self.scopes[-1][0] if self.scopes else self.top

    def sb(self, name, shape, dt):
        self.uid += 1
        t = self._stack().enter_context(self.nc.sbuf_tensor(f"{name}_{self.uid}", list(shape), dt))
        b = Buf(name)
        if self.scopes:
            self.scopes[-1][1].append(b)
        return Tl(t, b)

    def ps(self, name, shape, dt):
        t = self.top.enter_context(self.nc.psum_tensor(name, list(shape), dt))
        return Tl(t, Buf(name))

    def phase(self):
        prog = self

        class _Ph:
            def __enter__(s):
                prog.scopes.append((ExitStack(), []))
                return s

            def __exit__(s, *a):
                prog.barrier()
                es, bufs = prog.scopes.pop()
                for b in bufs:
                    if b.sem is not None:
                        prog.sem_free.append(b.sem)
                        if b in prog.live_dma_bufs:
                            prog.live_dma_bufs.remove(b)
                        b.sem = None
                es.close()
                return False
        return _Ph()

    def _get_sem(self, b):
        if b.sem is None:
            if self.sem_free:
                b.sem = self.sem_free.pop()
            else:
                s = self.top.enter_context(self.nc.semaphore(f"dq{len(self.sem_all)}"))
                b.sem = [s, 0]
                self.sem_all.append(b.sem)
            self.live_dma_bufs.append(b)
        return b.sem

    def add(self, eng, fn, r=(), w=(), dma_buf=None, inc=None, extra=()):
        op = Op()
        op.eng = eng
        op.fn = fn
        op.sig = False
        op.sig_idx = None
        op.waits = None
        op.dma_buf = dma_buf
        op.dma_val = None
        op.inc = inc
        op.extra = list(extra)
        op.force = []
        op.gidx = self.n
        self.n += 1
        deps = set()
        wset = [b for b in w if b is not None]
        rset = [b for b in r if b is not None and b not in wset]
        for b in rset:
            if b.last_w is not None:
                deps.add(b.last_w)
        for b in wset:
            if b.last_w is not None:
                deps.add(b.last_w)
            deps.update(b.readers)
        if self.pending[eng] is not None:
            deps.update(self.pending[eng])
            self.pending[eng] = None
        for b in rset:
            if dma_buf is None:
                b.readers = [o for o in b.readers if o.eng != eng or o.dma_buf is not None]
            b.readers.append(op)
        for b in wset:
            b.last_w = op
            b.readers = []
        if dma_buf is not None:
            sem = self._get_sem(dma_buf)
            sem[1] += 16
            op.dma_val = sem[1]
            op.inc = (sem[0], 16)
            dma_buf.last_dma = op
        op.deps = deps
        self.ops[eng].append(op)
        if dma_buf is None and inc is None and fn is not None:
            self.last[eng] = op
        return op

    def barrier(self):
        deps = [o for o in self.last.values() if o is not None]
        for b in self.live_dma_bufs:
            if b.last_dma is not None:
                deps.append(b.last_dma)
        for e in ENGS:
            cur = self.pending[e]
            self.pending[e] = list(deps) if cur is None else list(set(cur) | set(deps))

    def finalize(self, eng_sems):
        for e in ENGS:
            for op in self.ops[e]:
                for d in op.force:
                    d.sig = True
                for d in op.deps:
                    if d.dma_buf is None and d.inc is None:
                        if d.eng == "pe" and op.eng == "pe":
                            continue
                        if d.eng == op.eng and not self.ses:
                            continue
                        d.sig = True
        for e in ENGS:
            c = 0
            for op in self.ops[e]:
                if op.sig:
                    c += 1
                    op.sig_idx = c
        for e in ENGS:
            known = {}
            for op in self.ops[e]:
                want = {}
                for d in op.deps:
                    if d.dma_buf is not None:
                        key, val = d.inc[0], d.dma_val
                    elif d.inc is not None:
                        continue
                    else:
                        if d.eng == "pe" and op.eng == "pe":
                            continue
                        if d.eng == op.eng and not self.ses:
                            continue
                        key, val = eng_sems[d.eng], d.sig_idx
                    if val > want.get(id(key), (None, 0))[1]:
                        want[id(key)] = (key, val)
                for (key, val) in op.extra:
                    if val > want.get(id(key), (None, 0))[1]:
                        want[id(key)] = (key, val)
                for d in op.force:
                    key, val = eng_sems[d.eng], d.sig_idx
                    if val > want.get(id(key), (None, 0))[1]:
                        want[id(key)] = (key, val)
                ws = []
                for k, (key, val) in want.items():
                    if known.get(k, 0) < val:
                        known[k] = val
                        ws.append((key, val))
                op.waits = ws

    def emit(self, eng, e, eng_sems):
        for op in self.ops[eng]:
            for (s, v) in op.waits:
                e.wait_ge(s, v)
            if op.fn is None:
                continue
            ins = op.fn(e)
            if op.inc is not None:
                ins.then_inc(op.inc[0], op.inc[1])
            elif op.sig:
                ins.then_inc(eng_sems[eng], 1)

    def mm(self, out, lhsT, rhs, start=True, stop=True, f32=False):
        prev = self.last["pe"]
        op = self.add("pe", lambda e: e.matmul(out.ap, lhsT.ap, rhs.ap, start=start, stop=stop),
                      r=[lhsT.b, rhs.b], w=[out.b])
        if self.pe_fence is not None:
            op.force.append(self.pe_fence)
            self.pe_fence = None
        if f32:
            if prev is not None:
                op.force.append(prev)
            self.pe_fence = op
        return op

    def tr(self, out, in_, ident):
        op = self.add("pe", lambda e: e.transpose(out.ap, in_.ap, ident.ap), r=[in_.b, ident.b], w=[out.b])
        if self.pe_fence is not None:
            op.force.append(self.pe_fence)
            self.pe_fence = None
        return op

    def act(self, out, in_, func, bias=None, scale=None):
        rb = [in_.b]
        kw = {}
        if bias is not None:
            if isinstance(bias, V):
                rb.append(bias.b)
                kw["bias"] = bias.ap
            else:
                kw["bias"] = float(bias)
        if scale is not None:
            if isinstance(scale, V):
                rb.append(scale.b)
                kw["scale"] = scale.ap
            else:
                kw["scale"] = float(scale)
        return self.add("act", lambda e: e.activation(out.ap, in_.ap, func, **kw), r=rb, w=[out.b])

    def tt(self, out, a, b, op, eng="dve"):
        return self.add(eng, lambda e: e.tensor_tensor(out.ap, a.ap, b.ap, op), r=[a.b, b.b], w=[out.b])

    def ts(self, out, a, s1, op0, s2=None, op1=None, eng="dve"):
        rb = [a.b]
        x1 = s1
        if isinstance(s1, V):
            rb.append(s1.b)
            x1 = s1.ap
        x2 = s2
        if isinstance(s2, V):
            rb.append(s2.b)
            x2 = s2.ap
        if op1 is None:
            return self.add(eng, lambda e: e.tensor_scalar(out.ap, a.ap, x1, None, op0), r=rb, w=[out.b])
        return self.add(eng, lambda e: e.tensor_scalar(out.ap, a.ap, x1, x2, op0, op1), r=rb, w=[out.b])

    def stt(self, out, a, s, b, op0, op1):
        rb = [a.b, b.b]
        x = s
        if isinstance(s, V):
            rb.append(s.b)
            x = s.ap
        return self.add("dve", lambda e: e.scalar_tensor_tensor(out.ap, a.ap, x, b.ap, op0, op1), r=rb, w=[out.b])

    def cp(self, out, in_, eng="dve"):
        if eng == "act":
            return self.add("act", lambda e: e.activation(out.ap, in_.ap, AF.Copy), r=[in_.b], w=[out.b])
        return self.add(eng, lambda e: e.tensor_copy(out.ap, in_.ap), r=[in_.b], w=[out.b])

    def memset(self, out, val, eng="dve"):
        return self.add(eng, lambda e: e.memset(out.ap, val), w=[out.b])

    def recip(self, out, in_):
        return self.add("dve", lambda e: e.reciprocal(out.ap, in_.ap), r=[in_.b], w=[out.b])

    def scan(self, out, d0, d1, init, op0, op1):
        rb = [d0.b, d1.b]
        x = init
        if isinstance(init, V):
            rb.append(init.b)
            x = init.ap
        return self.add("dve", lambda e: e.tensor_tensor_scan(out.ap, d0.ap, d1.ap, x, op0, op1), r=rb, w=[out.b])

    def dma(self, out, in_, extra=(), q="sp"):
        if out.b is not None:
            return self.add(q, lambda e: e.dma_start(out=out.ap, in_=in_.ap), r=[in_.b], w=[out.b],
                            dma_buf=out.b, extra=extra)
        return self.add(q, lambda e: e.dma_start(out=out.ap, in_=in_.ap), r=[in_.b], w=[],
                        dma_buf=in_.b, extra=extra)


def build_program(NC, T, D, DFF, L, debug=False):
    KC = D // 128
    FC = DFF // 128
    TT = min(512, T)
    NT = T // TT
    NB = T // BLK
    BPT = TT // BLK
    SEG = 16 if FC >= 16 else FC
    NSEG = FC // SEG
    nc = bass.Bass("TRN2", target_bir_lowering=False)
    P = Prog(nc)

    def din(name, shape, dt=F32):
        return nc.dram_tensor(name, list(shape), dt, kind="ExternalInput").ap()

    def dscr(name, shape, dt):
        return nc.dram_tensor(name, list(shape), dt, kind="Internal").ap()

    x_in = din("xT", [D, T])
    memT_in = din("memT", [D, MEM])
    CO = {}
    ncol = 0

    def col(name, n):
        nonlocal ncol
        CO[name] = ncol
        ncol += n
    col("norm_mix", L * KC)
    col("norm_ffn", L * KC)
    col("final_norm", KC)
    col("mem_norm", KC)
    col("b_gate", L * 3 * KC)
    col("gla_norm", L * 8)
    col("ml_norm", L * 8)
    col("nb_gla_a", L * 4)
    col("conv_w", L * 4 * 8)
    col("conv_b", L * 8)
    col("sel", 8)
    col("selprev", 8)
    cols_in = din("cols", [128, ncol])
    consts_in = din("consts", [128, 3 * 128])
    g4_in = din("g4", [4, 4 * 128 + 4 * L + 8])
    g4m_in = din("g4m", [4, 2 * T])
    wga_in = din("wga", [RANK, L * 512])
    WNAMES = ["w_in", "w_mk", "w_mv", "w_bg", "w_out", "w_ff1", "w_ff2"]
    WSHAPE = {"w_in": (D, IN_W), "w_mk": (D, MIXV), "w_mv": (D, MIXV), "w_bg": (3 * MIXV + 3 * 256, D),
              "w_out": (D, D), "w_ff1": (D, DFF), "w_ff2": (DFF, D)}
    w_sh_in = {}
    w_sh_bf = {}
    w_full = {}
    for l in range(L):
        for wn in WNAMES:
            R, C = WSHAPE[wn]
            w_sh_in[(l, wn)] = din(f"{wn}_{l}", [R // NC, C])
            w_sh_bf[(l, wn)] = dscr(f"{wn}_sb_{l}", [R // NC, C], BF16)
            w_full[(l, wn)] = dscr(f"{wn}_f_{l}", [R, C], BF16)
    okind = "ExternalOutput"
    outT = nc.dram_tensor("outT", [D, T], F32, kind=okind).ap()

    dk_ = "ExternalOutput" if debug else "Internal"

    def dbg(name, shape, dt):
        return nc.dram_tensor(name, list(shape), dt, kind=dk_).ap()
    xs = dscr("x_scr", [D, T], F32)
    ZF = 5400
    zf = dbg("zf", [ZF, T], F32)
    zc = dbg("zc", [1024, T], F32)
    vtok = dbg("vtok", [T, 2048], BF16)
    obr = dbg("obr", [3 * MIXV, T], BF16)
    halo_src = dscr("halo_src", [1024, 8], F32)
    halo_all = dscr("halo_all", [NC * 1024, 8], F32)
    XR = 1152
    XW = 264
    xch_src = dscr("xch_src", [XR, XW], F32)
    xch_all = dscr("xch_all", [NC * XR, XW], F32)
    rg = [list(range(NC))]

    es_ = P.top
    eng_sems = {e: es_.enter_context(nc.semaphore(f"eng_{e}")) for e in ENGS}
    sem_cast = es_.enter_context(nc.semaphore("wcast"))
    sem_ag = es_.enter_context(nc.semaphore("wag"))
    sem_cc = es_.enter_context(nc.semaphore("cc"))
    ag_count = [0]
    cast_count = [0]
    cc_count = [0]
    w_ready = {}

    colsb = P.sb("cols", [128, ncol], F32)
    cst = P.sb("consts", [128, 384], F32)
    maskT = cst[:, 0:128]
    ones32 = cst[:, 256:384]
    identb = P.sb("identb", [128, 128], BF16)
    onesb = P.sb("onesb", [128, 128], BF16)
    g4 = P.sb("g4", [4, 4 * 128 + 4 * L + 8], F32)
    G4_B = 512
    G4_SEL = G4_B + 4 * L
    memn_d = dscr("memn_d", [D, MEM], BF16)
    epsc = P.sb("epsc", [128, 1], F32)
    PSB = [P.ps(f"ps{i}", [128, 512], F32) for i in range(7)]
    PST = P.ps("pst", [128, 1024], BF16)

    def C_(name, i=0):
        j = CO[name] + i
        return colsb[:, j:j + 1]

    def prep_weights(l, names=None):
        for wn in (names or WNAMES):
            src, sb_, full = w_sh_in[(l, wn)], w_sh_bf[(l, wn)], w_full[(l, wn)]
            cast_count[0] += 16
            P.add("pool", (lambda e, o=sb_, i=src: e.dma_start(out=o, in_=i)), inc=(sem_cast, 16))
            ag_count[0] += 1
            P.add("pool", (lambda e, o=full, i=sb_: e.collective_compute(
                "AllGather", ALU.bypass, replica_groups=rg, ins=[i], outs=[o])),
                inc=(sem_ag, 1), extra=[(sem_cast, cast_count[0])])
            P.add("pool", None, extra=[(sem_ag, ag_count[0])])
            w_ready[(l, wn)] = ag_count[0]

    def collective_ag(src_ap, dst_ap):
        P.barrier()
        cc_count[0] += 1
        P.add("pool", (lambda e: e.collective_compute("AllGather", ALU.bypass, replica_groups=rg,
                                                      ins=[src_ap], outs=[dst_ap])), inc=(sem_cc, 1))
        for e in ENGS:
            P.add(e, None, extra=[(sem_cc, cc_count[0])])

    P.dma(colsb[:, :], V(cols_in, None))
    P.dma(cst[:, :], V(consts_in, None))
    P.dma(g4[:, :], V(g4_in, None))
    P.cp(identb[:, :], cst[:, 128:256])
    P.memset(onesb[:, :], 1.0)
    P.memset(epsc[:, :], EPS)

    prep_weights(0)
    PREP_A = ["w_in", "w_mk", "w_mv", "w_bg", "w_out"]
    PREP_B = ["w_ff1", "w_ff2"]
    SPLIT_PREP = False
    if L > 1 and SPLIT_PREP:
        prep_weights(1, PREP_A)

    def lockstep(gens):
        gens = list(gens)
        while gens:
            for g in list(gens):
                try:
                    next(g)
                except StopIteration:
                    gens.remove(g)

    wslot_n = [0]

    def rms_scale(ps, tmp_std, rstd, n, width):
        P.act(tmp_std, ps, AF.Sqrt, bias=epsc[:, 0:1], scale=1.0 / n)
        P.recip(rstd, tmp_std)

    def load_panel(wslots, l, wn, r0, nk, c0, pw):
        s = wslots[wslot_n[0] % len(wslots)]
        wslot_n[0] += 1
        full = w_full[(l, wn)]
        src = full[r0:r0 + nk * 128, c0:c0 + pw].rearrange("(k p) n -> p k n", p=128)
        view = s.t[:, 0:nk * pw].rearrange("p (k n) -> p k n", n=pw)
        P.dma(V(view, s.b), V(src, None), extra=[(sem_ag, w_ready[(l, wn)])])
        return V(view, s.b)

    with P.phase():
        mt = P.sb("mt", [128, KC, MEM], F32)
        acc = P.sb("macc", [128, MEM], F32)
        sq = P.sb("msq", [128, MEM], F32)
        std = P.sb("mstd", [128, MEM], F32)
        rstd = P.sb("mrstd", [128, MEM], F32)
        P.dma(mt[:, :, :], V(memT_in.rearrange("(k p) m -> p k m", p=128), None))
        for k in range(KC):
            if k == 0:
                P.tt(acc[:, :], mt[:, k, :], mt[:, k, :], ALU.mult)
            else:
                P.tt(sq[:, :], mt[:, k, :], mt[:, k, :], ALU.mult)
                P.tt(acc[:, :], acc[:, :], sq[:, :], ALU.add)
        P.mm(PSB[0][:, 0:MEM], ones32, acc[:, :], f32=True)
        rms_scale(PSB[0][:, 0:MEM], std[:, :], rstd[:, :], D, MEM)
        memnT0 = P.sb("memnT0", [128, KC, MEM], BF16)
        for k in range(KC):
            P.stt(memnT0[:, k, :], mt[:, k, :], C_("mem_norm", k), rstd[:, :], ALU.mult, ALU.mult)
        P.dma(V(memn_d.rearrange("(k p) m -> p k m", p=128), None), memnT0[:, :, :])

    def phase_inproj(l):
        xsrc = x_in if l == 0 else xs
        with P.phase():
            hTs = [P.sb(f"hT{i}", [128, KC, TT], BF16) for i in range(2 if NT > 1 else 1)]
            xin = [P.sb(f"xin{i}", [128, TT], F32) for i in range(3)]
            sq = P.sb("sq", [128, TT], F32)
            acc = P.sb("acc", [128, TT], F32)
            std = P.sb("std", [128, TT], F32)
            rstd = P.sb("rstd", [128, TT], F32)
            zst = [P.sb(f"zst{i}", [128, TT], F32) for i in range(3)]
            vst = [P.sb(f"vst{i}", [128, 512], BF16) for i in range(2)]
            wsl = [P.sb(f"wA{i}", [128, KC * 512], BF16) for i in range(2)]
            evc = [0]

            def rms(j):
                hT = hTs[j % len(hTs)]
                tsl = slice(j * TT, (j + 1) * TT)
                for k in range(KC):
                    xi = xin[k % 3]
                    P.dma(xi[:, :], V(xsrc[k * 128:(k + 1) * 128, tsl], None))
                    if k == 0:
                        P.act(acc[:, :], xi[:, :], AF.Square)
                    else:
                        P.act(sq[:, :], xi[:, :], AF.Square)
                        P.tt(acc[:, :], acc[:, :], sq[:, :], ALU.add)
                    yield
                P.mm(PSB[4][:, 0:TT], ones32, acc[:, :], f32=True)
                rms_scale(PSB[4][:, 0:TT], std[:, :], rstd[:, :], D, TT)
                yield
                for k in range(KC):
                    xi = xin[k % 3]
                    P.dma(xi[:, :], V(xsrc[k * 128:(k + 1) * 128, tsl], None))
                    P.stt(hT[:, k, :], xi[:, :], C_("norm_mix", l * KC + k), rstd[:, :], ALU.mult, ALU.mult)
                    yield

            def gemm(j):
                hT = hTs[j % len(hTs)]
                tsl = slice(j * TT, (j + 1) * TT)
                groups = [(0, 1024, 0), (2048, 1024, 1024), (3088, 1024, 2048), (5136, 1024, 3072),
                          (6168, 1024, 4096), (7192, 256, 5120), (3072, 16, 5376), (6160, 4, 5392),
                          (6164, 4, 5396)]
                for (c0, ncols, z0) in groups:
                    for p0 in range(0, ncols, 512):
                        pw = min(512, ncols - p0)
                        wv = load_panel(wsl, l, "w_in", 0, KC, c0 + p0, pw)
                        for m0 in range(0, pw, 128):
                            M = min(128, pw - m0)
                            ev = evc[0]
                            evc[0] += 1
                            ps = PSB[ev % 4]
                            for k in range(KC):
                                P.mm(ps[0:M, 0:TT], V(wv.ap[:, k, m0:m0 + M], wv.b), hT[:, k, :],
                                     start=(k == 0), stop=(k == KC - 1))
                            st = zst[ev % 3]
                            P.cp(st[0:M, :], ps[0:M, 0:TT], eng="act" if ev % 2 == 0 else "dve")
                            P.dma(V(zf[z0 + p0 + m0: z0 + p0 + m0 + M, tsl], None), st[0:M, :])
                            yield
                for (c0, v0) in ((1024, 0), (4112, 1024)):
                    for p0 in range(0, 1024, 512):
                        wv = load_panel(wsl, l, "w_in", 0, KC, c0 + p0, 512)
                        for tb in range(BPT):
                            ev = evc[0]
                            evc[0] += 1
                            ps = PSB[ev % 4]
                            for k in range(KC):
                                P.mm(ps[:, :], hT[:, k, tb * BLK:(tb + 1) * BLK], V(wv.ap[:, k, :], wv.b),
                                     start=(k == 0), stop=(k == KC - 1))
                            st = vst[ev % 2]
                            P.cp(st[:, :], ps[:, :], eng="act" if ev % 2 == 0 else "dve")
                            r0 = j * TT + tb * BLK
                            P.dma(V(vtok[r0:r0 + BLK, v0 + p0:v0 + p0 + 512], None), st[:, :])
                            yield

            if True:
                lockstep([rms(0)])
                for j in range(NT):
                    gs = [gemm(j)]
                    if j + 1 < NT:
                        gs.append(rms(j + 1))
                    lockstep(gs)
            else:
                for j in range(NT):
                    lockstep([rms(j)])
                    lockstep([gemm(j)])

    def phase_memkv(l, kmT, vm):
        with P.phase():
            wsl = [P.sb(f"wM{i}", [128, KC * 512], BF16) for i in range(2)]
            memnT = P.sb("memnT", [128, KC, MEM], BF16)
            P.dma(memnT[:, :, :], V(memn_d.rearrange("(k p) m -> p k m", p=128), None))
            ev = 0
            for p0 in range(0, MIXV, 512):
                wv = load_panel(wsl, l, "w_mk", 0, KC, p0, 512)
                for m0 in range(0, 512, 128):
                    ps = PSB[ev % 4]
                    for k in range(KC):
                        P.mm(ps[:, 0:MEM], V(wv.ap[:, k, m0:m0 + 128], wv.b), memnT[:, k, :],
                             start=(k == 0), stop=(k == KC - 1))
                    P.cp(kmT[:, (p0 + m0) // 128, :], ps[:, 0:MEM], eng="act" if ev % 2 else "dve")
                    ev += 1
            for p0 in range(0, MIXV, 512):
                wv = load_panel(wsl, l, "w_mv", 0, KC, p0, 512)
                for mc in range(MEM // 128):
                    ps = PSB[ev % 4]
                    for k in range(KC):
                        P.mm(ps[:, :], memnT[:, k, mc * 128:(mc + 1) * 128], V(wv.ap[:, k, :], wv.b),
                             start=(k == 0), stop=(k == KC - 1))
                    P.cp(vm[:, mc, p0:p0 + 512], ps[:, :], eng="act" if ev % 2 else "dve")
                    ev += 1

    def phase_memattn(l, kmT, vm):
        with P.phase():
            xq32 = [P.sb(f"xq32_{i}", [128, 2, TT], F32) for i in range(2)]
            xqb = [P.sb(f"xqb{i}", [128, 2, TT], BF16) for i in range(2)]
            pT = [P.sb(f"pT{i}", [128, 2, TT], BF16) for i in range(2)]
            rden = [P.sb(f"rden{i}", [128, TT], F32) for i in range(2)]
            ost = [P.sb(f"ost{i}", [128, TT], BF16) for i in range(3)]
            it = 0
            oi = 0
            for j in range(NT):
                tsl = slice(j * TT, (j + 1) * TT)
                for h in range(NH):
                    a = it % 2
                    it += 1
                    src = zf[4096 + h * 256:4096 + (h + 1) * 256, tsl].rearrange("(c p) t -> p c t", p=128)
                    P.dma(xq32[a][:, :, :], V(src, None))
                    P.cp(xqb[a][:, :, :], xq32[a][:, :, :])
                    for mc in range(2):
                        ps = PSB[mc]
                        for c in range(2):
                            P.mm(ps[:, 0:TT], kmT[:, h * 2 + c, mc * 128:(mc + 1) * 128], xqb[a][:, c, :],
                                 start=(c == 0), stop=(c == 1))
                        P.act(pT[a][:, mc, :], ps[:, 0:TT], AF.Exp, scale=DV ** -0.5)
                    for mc in range(2):
                        P.mm(PSB[2][:, 0:TT], onesb[:, :], pT[a][:, mc, :], start=(mc == 0), stop=(mc == 1))
                    P.recip(rden[a][:, :], PSB[2][:, 0:TT])
                    for c in range(2):
                        ps = PSB[3 + c]
                        for mc in range(2):
                            P.mm(ps[:, 0:TT], vm[:, mc, h * 256 + c * 128:h * 256 + (c + 1) * 128], pT[a][:, mc, :],
                                 start=(mc == 0), stop=(mc == 1))
                        o = ost[oi % 3]
                        oi += 1
                        P.tt(o[:, :], ps[:, 0:TT], rden[a][:, :], ALU.mult)
                        r0 = 2 * MIXV + h * 256 + c * 128
                        P.dma(V(obr[r0:r0 + 128, tsl], None), o[:, :])

    def phase_conv(l):
        with P.phase():
            hs = P.sb("hs", [128, 8, 8], F32)
            P.memset(hs[:, :, :], 0.0)
            src = zf[2048:3072, T - 3:T].rearrange("(c p) t -> p c t", p=128)
            P.dma(hs[:, :, 0:3], V(src, None))
            P.barrier()
            P.dma(V(halo_src.rearrange("(c p) w -> p c w", p=128), None), hs[:, :, :])
        collective_ag(halo_src, halo_all)
        with P.phase():
            ha = P.sb("ha", [128, NC, 8, 8], F32)
            hh = P.sb("hh", [128, 8, 8], F32)
            P.dma(ha[:, :, :, :], V(halo_all.rearrange("(r c p) w -> p r c w", p=128, c=8), None))
            P.memset(hh[:, :, :], 0.0)
            for r in range(NC):
                P.stt(hh[:, :, :], ha[:, r, :, :], C_("selprev", r), hh[:, :, :], ALU.mult, ALU.add)
            u = [P.sb(f"cu{i}", [128, T + 3], F32) for i in range(2)]
            y = [P.sb(f"cy{i}", [128, T], F32) for i in range(2)]
            for c in range(8):
                a = c % 2
                P.dma(u[a][:, 3:T + 3], V(zf[2048 + c * 128:2048 + (c + 1) * 128, :], None))
                P.cp(u[a][:, 0:3], hh[:, c, 0:3])
                P.ts(y[a][:, :], u[a][:, 0:T], C_("conv_w", (l * 4 + 0) * 8 + c), ALU.mult,
                     C_("conv_b", l * 8 + c), ALU.add)
                for jx in range(1, 4):
                    P.stt(y[a][:, :], u[a][:, jx:jx + T], C_("conv_w", (l * 4 + jx) * 8 + c), y[a][:, :],
                          ALU.mult, ALU.add)
                P.act(y[a][:, :], y[a][:, :], AF.Silu)
                P.dma(V(zc[c * 128:(c + 1) * 128, :], None), y[a][:, :])

    def mixer_pass(l, final, st):
        S, Sb, Cn, Cb, nbc, Dtot = st["S"], st["Sb"], st["Cn"], st["Cb"], st["nbc"], st["Dtot"]
        with P.phase():
            mi = P.sb("mi", [4, T], F32)
            mf = P.sb("mf", [4, T], F32)
            sp_ = P.sb("sp", [4, T], F32)
            cs = P.sb("cs", [4, T], F32)
            aa = P.sb("aa", [4, T], F32)
            pm = P.sb("pm", [4, T], F32)
            tmp4 = P.sb("tmp4", [4, T], F32)
            G = P.sb("G", [4, NB, 4, BLK], F32)
            AB = P.sb("AB", [4, NB, 2], F32)
            gl_ = P.sb("gl", [4, NB], F32)
            mnx = P.sb("mnx", [4, NB + 1], F32)
            g4m = P.sb("g4m", [4, 2 * T], F32)
            P.dma(g4m[:, :], V(g4m_in, None))
            wga = P.sb("wga", [RANK, 512], F32)
            P.dma(wga[:, :], V(wga_in[:, l * 512:(l + 1) * 512], None))
            P.dma(mi[:, :], V(zf[5392:5396, :], None))
            P.dma(mf[:, :], V(zf[5396:5400, :], None))
            nbf = g4[:, G4_B + 4 * l + 0:G4_B + 4 * l + 1]
            bi = g4[:, G4_B + 4 * l + 1:G4_B + 4 * l + 2]
            P.act(sp_[:, :], mf[:, :], AF.Exp, bias=nbf, scale=-1.0)
            P.act(sp_[:, :], sp_[:, :], AF.Ln, bias=1.0)
            P.scan(cs[:, :], g4m[:, 0:T], sp_[:, :], 0.0, ALU.mult, ALU.add)
            P.stt(aa[:, :], mi[:, :], bi, cs[:, :], ALU.add, ALU.add)
            P.scan(pm[:, :], g4m[:, T:2 * T], aa[:, :], NEG, ALU.add, ALU.max)
            csl = V(cs.t[:, BLK - 1::BLK], cs.b)
            pml = V(pm.t[:, BLK - 1::BLK], pm.b)
            P.tt(gl_[:, :], pml, csl, ALU.subtract)
            P.ts(tmp4[:, 0:NB], csl, -1.0, ALU.mult)
            P.cp(mnx[:, 0:1], st["m0"][:, 0:1])
            P.scan(mnx[:, 1:NB + 1], tmp4[:, 0:NB], gl_[:, :], st["m0"][:, 0:1], ALU.add, ALU.max)
            P.tt(tmp4[:, NB:2 * NB], tmp4[:, 0:NB], mnx[:, 1:NB + 1], ALU.subtract)
            P.act(V(AB.t[:, :, 1], AB.b), tmp4[:, NB:2 * NB], AF.Exp)
            P.tt(tmp4[:, NB:2 * NB], tmp4[:, NB:2 * NB], mnx[:, 0:NB], ALU.add)
            P.act(V(AB.t[:, :, 0], AB.b), tmp4[:, NB:2 * NB], AF.Exp)
            lnk = P.sb("lnk", [4, 1], F32)
            P.memset(lnk[:, :], -0.5 * math.log(DK))
            for b in range(NB):
                bs = slice(b * BLK, (b + 1) * BLK)
                P.act(G[:, b, 0, :], aa[:, bs], AF.Exp, bias=lnk[:, 0:1])
                P.ts(tmp4[:, bs], pm[:, bs], mnx[:, b:b + 1], ALU.max)
                P.act(G[:, b, 1, :], tmp4[:, bs], AF.Exp, scale=-1.0)
                P.tt(sp_[:, bs], cs[:, bs], tmp4[:, bs], ALU.subtract)
                P.act(G[:, b, 3, :], sp_[:, bs], AF.Exp)
                P.ts(tmp4[:, bs], tmp4[:, bs], mnx[:, b:b + 1], ALU.subtract)
                P.act(G[:, b, 2, :], tmp4[:, bs], AF.Exp, scale=-1.0)
            if not final:
                P.cp(st["scal"][:, 0:1], mnx[:, NB:NB + 1])
                P.add("dve", lambda e: e.tensor_reduce(st["scal"].t[:, 1:2], csl.ap, mybir.AxisListType.X, ALU.add),
                      r=[cs.b], w=[st["scal"].b])
                P.ts(st["scal"][:, 1:2], st["scal"][:, 1:2], -1.0, ALU.mult)

            qk = [P.sb(f"qk{i}", [128, 16, BLK], F32) for i in range(2)]
            gab = [P.sb(f"ga{i}", [RANK, BLK], F32) for i in range(2)]
            vb = [P.sb(f"vb{i}", [128, 2048], BF16) for i in range(2)]
            gat = [P.sb(f"gat{i}", [128, 16, BLK], F32) for i in range(2)] if final else None
            la = [P.sb(f"la{i}", [128, BLK], F32) for i in range(2)]
            cum = [P.sb(f"cum{i}", [128, BLK], F32) for i in range(2)]
            eb = [P.sb(f"eb{i}", [128, BLK], F32) for i in range(2)]
            enb = [P.sb(f"enb{i}", [128, BLK], F32) for i in range(2)]
            bcs = [P.sb(f"bcs{i}", [128, 512], F32) for i in range(2)]
            abs_ = [P.sb(f"abs{i}", [128, 2], F32) for i in range(2)]
            qd = [P.sb(f"qd{i}", [128, BLK], BF16) for i in range(2)]
            qh = [P.sb(f"qh{i}", [128, BLK], BF16) for i in range(2)]
            kd = [P.sb(f"kd{i}", [128, BLK], BF16) for i in range(2)]
            kdt = [P.sb(f"kdt{i}", [128, BLK], BF16) for i in range(2)]
            att = [P.sb(f"att{i}", [128, BLK], BF16) for i in range(2)]
            tmpS = [P.sb(f"tmpS{i}", [128, DV + 1], F32) for i in range(2)]
            o32 = [P.sb(f"o32{i}", [128, 2, BLK], F32) for i in range(2)]
            osq = [P.sb(f"osq{i}", [128, 2, BLK], BF16) for i in range(2)]
            dn = [P.sb(f"dn{i}", [128, BLK], F32) for i in range(2)]
            sg = [P.sb(f"sg{i}", [128, 2, BLK], F32) for i in range(2)]
            oo = [P.sb(f"oo{i}", [128, 2, BLK], BF16) for i in range(3)]
            onesc = P.sb("onesc", [128, 1], BF16)
            P.memset(onesc[:, :], 1.0)
            oi = 0
            for b in range(NB):
                a = b % 2
                bs = slice(b * BLK, (b + 1) * BLK)
                P.dma(qk[a][:, 0:8, :], V(zf[0:1024, bs].rearrange("(c p) t -> p c t", p=128), None))
                P.dma(qk[a][:, 8:16, :], V(zc[0:1024, bs].rearrange("(c p) t -> p c t", p=128), None))
                P.dma(gab[a][:, :], V(zf[5376:5392, bs], None))
                P.dma(vb[a][:, :], V(vtok[bs, :], None))
                if final:
                    P.dma(gat[a][:, 0:8, :], V(zf[1024:2048, bs].rearrange("(c p) t -> p c t", p=128), None))
                    P.dma(gat[a][:, 8:16, :], V(zf[3072:4096, bs].rearrange("(c p) t -> p c t", p=128), None))
                for h in range(NH):
                    p = h % 2
                    X, Y, Z = PSB[3 * p], PSB[3 * p + 1], PSB[3 * p + 2]
                    P.mm(X[:, 0:BLK], wga[:, h * 128:(h + 1) * 128], gab[a][:, :], f32=True)
                    P.act(la[p][:, :], X[:, 0:BLK], AF.Exp, bias=C_("nb_gla_a", l * 4 + h), scale=-1.0)
                    P.act(la[p][:, :], la[p][:, :], AF.Ln, bias=1.0)
                    P.scan(cum[p][:, :], ones32, la[p][:, :], 0.0, ALU.mult, ALU.add)
                    P.act(eb[p][:, :], cum[p][:, :], AF.Exp, scale=-1.0 / 16.0)
                    P.act(enb[p][:, :], cum[p][:, :], AF.Exp, scale=1.0 / 16.0)
                    P.tt(kd[p][:, :], qk[a][:, 4 + h, :], enb[p][:, :], ALU.mult)
                    P.tr(PST[:, 0:BLK], kd[p][:, :], identb[:, :])
                    P.cp(kdt[p][:, :], PST[:, 0:BLK], eng="act")
                    if final:
                        P.stt(qd[p][:, :], qk[a][:, h, :], DK ** -0.5, eb[p][:, :], ALU.mult, ALU.mult)
                        P.mm(Y[:, 0:BLK], kd[p][:, :], qd[p][:, :])
                        P.tt(att[p][:, :], Y[:, 0:BLK], maskT, ALU.mult)
                        for c in range(2):
                            P.mm(Z[:, c * BLK:(c + 1) * BLK], vb[a][:, h * DV + c * 128:h * DV + (c + 1) * 128],
                                 att[p][:, :], start=True, stop=False)
                            P.mm(Z[:, c * BLK:(c + 1) * BLK], Sb[h][:, c * 128:(c + 1) * 128], qd[p][:, :],
                                 start=False, stop=True)
                        P.cp(V(o32[p].t[:, :, :], o32[p].b),
                             V(Z.t[:, 0:2 * BLK].rearrange("p (c t) -> p c t", c=2), Z.b), eng="act")
                        P.tt(osq[p][:, :, :], o32[p][:, :, :], o32[p][:, :, :], ALU.mult)
                        for c in range(2):
                            P.mm(Z[:, 2 * BLK:3 * BLK], onesb[:, :], osq[p][:, c, :], start=(c == 0), stop=(c == 1))
                        P.act(dn[p][:, :], Z[:, 2 * BLK:3 * BLK], AF.Sqrt, bias=epsc[:, 0:1], scale=1.0 / DV)
                        P.recip(dn[p][:, :], dn[p][:, :])
                        P.act(sg[p][:, :, :], gat[a][:, 2 * h:2 * h + 2, :], AF.Silu)
                        o = oo[oi % 3]
                        oi += 1
                        for c in range(2):
                            P.stt(o32[p][:, c, :], o32[p][:, c, :], C_("gla_norm", l * 8 + 2 * h + c), dn[p][:, :],
                                  ALU.mult, ALU.mult)
                            P.tt(o[:, c, :], o32[p][:, c, :], sg[p][:, c, :], ALU.mult)
                        P.dma(V(obr[h * DV:(h + 1) * DV, bs].rearrange("(c p) t -> p c t", p=128), None), o[:, :, :])
                    P.mm(Y[:, 128:128 + DV], kdt[p][:, :], vb[a][:, h * DV:(h + 1) * DV])
                    P.tt(tmpS[p][:, 0:DV], S[h][:, :], Y[:, 128:128 + DV], ALU.add)
                    P.ts(S[h][:, :], tmpS[p][:, 0:DV], eb[p][:, BLK - 1:BLK], ALU.mult)
                    if final:
                        P.cp(Sb[h][:, :], S[h][:, :], eng="act")
                    else:
                        P.tt(Dtot[:, h:h + 1], Dtot[:, h:h + 1], eb[p][:, BLK - 1:BLK], ALU.mult)
                for h in range(NH):
                    p = h % 2
                    X, Y, Z = PSB[3 * p], PSB[3 * p + 1], PSB[3 * p + 2]
                    eh = g4[:, h * 128:(h + 1) * 128]
                    P.mm(X[:, :], eh, V(G.t[:, b, :, :].rearrange("p r t -> p (r t)"), G.b), f32=True)
                    P.cp(bcs[p][:, :], X[:, :], eng="act")
                    P.mm(PSB[6][:, 0:2], eh, AB[:, b, :], f32=True)
                    P.cp(abs_[p][:, :], PSB[6][:, 0:2], eng="act")
                    P.tt(kd[p][:, :], qk[a][:, 12 + h, :], bcs[p][:, 0:BLK], ALU.mult)
                    P.tr(PST[:, 0:BLK], kd[p][:, :], identb[:, :])
                    P.cp(kdt[p][:, :], PST[:, 0:BLK], eng="act")
                    if final:
                        P.tt(qd[p][:, :], qk[a][:, 8 + h, :], bcs[p][:, BLK:2 * BLK], ALU.mult)
                        P.tt(qh[p][:, :], qk[a][:, 8 + h, :], bcs[p][:, 2 * BLK:3 * BLK], ALU.mult)
                        P.mm(Y[:, 0:BLK], kd[p][:, :], qd[p][:, :])
                        P.tt(att[p][:, :], Y[:, 0:BLK], maskT, ALU.mult)
                        for c in range(2):
                            P.mm(Z[:, c * BLK:(c + 1) * BLK],
                                 vb[a][:, 1024 + h * DV + c * 128:1024 + h * DV + (c + 1) * 128],
                                 att[p][:, :], start=True, stop=False)
                            P.mm(Z[:, c * BLK:(c + 1) * BLK], Cb[h][:, c * 128:(c + 1) * 128], qh[p][:, :],
                                 start=False, stop=True)
                        P.mm(Z[:, 2 * BLK:3 * BLK], onesb[:, :], att[p][:, :], start=True, stop=False)
                        P.mm(Z[:, 2 * BLK:3 * BLK], nbc[h][:, :], qh[p][:, :], start=False, stop=True)
                        P.act(dn[p][:, :], Z[:, 2 * BLK:3 * BLK], AF.Abs)
                        P.tt(dn[p][:, :], dn[p][:, :], bcs[p][:, 3 * BLK:4 * BLK], ALU.max)
                        P.recip(dn[p][:, :], dn[p][:, :])
                        P.act(sg[p][:, :, :], gat[a][:, 8 + 2 * h:8 + 2 * h + 2, :], AF.Sigmoid)
                        for c in range(2):
                            P.tt(o32[p][:, c, :], Z[:, c * BLK:(c + 1) * BLK], dn[p][:, :], ALU.mult)
                        P.tt(o32[p][:, :, :], o32[p][:, :, :], sg[p][:, :, :], ALU.mult)
                        P.tt(osq[p][:, :, :], o32[p][:, :, :], o32[p][:, :, :], ALU.mult)
                        for c in range(2):
                            P.mm(Z[:, 3 * BLK:4 * BLK], onesb[:, :], osq[p][:, c, :], start=(c == 0), stop=(c == 1))
                        P.act(dn[p][:, :], Z[:, 3 * BLK:4 * BLK], AF.Sqrt, bias=epsc[:, 0:1], scale=1.0 / DV)
                        P.recip(dn[p][:, :], dn[p][:, :])
                        o = oo[oi % 3]
                        oi += 1
                        for c in range(2):
                            P.stt(o[:, c, :], o32[p][:, c, :], C_("ml_norm", l * 8 + 2 * h + c), dn[p][:, :],
                                  ALU.mult, ALU.mult)
                        P.dma(V(obr[MIXV + h * DV:MIXV + (h + 1) * DV, bs].rearrange("(c p) t -> p c t", p=128), None),
                              o[:, :, :])
                    P.mm(Y[:, 128:128 + DV], kdt[p][:, :], vb[a][:, 1024 + h * DV:1024 + (h + 1) * DV])
                    P.mm(Y[:, 128 + DV:128 + DV + 1], kdt[p][:, :], onesc[:, :])
                    P.ts(tmpS[p][:, :], Y[:, 128:128 + DV + 1], abs_[p][:, 1:2], ALU.mult)
                    P.stt(Cn[h][:, :], Cn[h][:, :], abs_[p][:, 0:1], tmpS[p][:, :], ALU.mult, ALU.add)
                    if final:
                        P.cp(Cb[h][:, :], Cn[h][:, 0:DV], eng="act")
                        P.ts(nbc[h][:, :], ones32, Cn[h][:, DV:DV + 1], ALU.mult)

    def phase_mixers(l):
        with P.phase():
            st = {
                "S": [P.sb(f"S{h}", [128, DV], F32) for h in range(NH)],
                "Sb": [P.sb(f"Sb{h}", [128, DV], BF16) for h in range(NH)],
                "Cn": [P.sb(f"Cn{h}", [128, DV + 1], F32) for h in range(NH)],
                "Cb": [P.sb(f"Cb{h}", [128, DV], BF16) for h in range(NH)],
                "nbc": [P.sb(f"nbc{h}", [128, 128], BF16) for h in range(NH)],
                "Dtot": P.sb("Dtot", [128, NH], F32),
                "m0": P.sb("m0", [4, 1], F32),
                "scal": P.sb("scal", [4, 2], F32),
            }
            for h in range(NH):
                P.memset(st["S"][h][:, :], 0.0)
                P.memset(st["Cn"][h][:, :], 0.0)
            P.memset(st["Dtot"][:, :], 1.0)
            P.memset(st["m0"][:, :], NEG)
            mixer_pass(l, False, st)
            pub = [P.sb(f"pub{i}", [128, XW], F32) for i in range(3)]
            pi_ = 0
            for h in range(NH):
                pb_ = pub[pi_ % 3]
                pi_ += 1
                P.memset(pb_[:, :], 0.0)
                P.cp(pb_[:, 0:DV], st["S"][h][:, :])
                P.cp(pb_[:, DV:DV + 1], st["Dtot"][:, h:h + 1])
                P.dma(V(xch_src[h * 128:(h + 1) * 128, :], None), pb_[:, :])
                pb_ = pub[pi_ % 3]
                pi_ += 1
                P.memset(pb_[:, :], 0.0)
                P.cp(pb_[:, 0:DV + 1], st["Cn"][h][:, :])
                P.dma(V(xch_src[512 + h * 128:512 + (h + 1) * 128, :], None), pb_[:, :])
            pb_ = pub[pi_ % 3]
            P.memset(pb_[:, :], 0.0)
            P.cp(pb_[0:4, 0:2], st["scal"][:, :])
            P.dma(V(xch_src[1024:1152, :], None), pb_[:, :])
            collective_ag(xch_src, xch_all)
            if SPLIT_PREP:
                if l + 1 < L:
                    prep_weights(l + 1, PREP_B)
                if l + 2 < L:
                    prep_weights(l + 2, PREP_A)
            elif l + 1 < L:
                prep_weights(l + 1)
            with P.phase():
                xa = xch_all.rearrange("(r q) w -> r q w", q=XR)
                sc = P.sb("sc", [4, NC, 2], F32)
                P.dma(sc[:, :, :], V(xa[:, 1024:1028, 0:2].rearrange("r h c -> h r c"), None))
                sel4 = g4[:, G4_SEL:G4_SEL + NC]
                fte = P.sb("fte", [4, NC], F32)
                mre = P.sb("mre", [4, NC], F32)
                t4 = P.sb("t4", [4, NC], F32)
                mseq = P.sb("mseq", [4, NC + 1], F32)
                ab = P.sb("ab", [4, NC, 2], F32)
                P.tt(fte[:, :], V(sc.t[:, :, 1], sc.b), sel4, ALU.mult)
                P.tt(mre[:, :], V(sc.t[:, :, 0], sc.b), sel4, ALU.mult)
                P.ts(t4[:, :], sel4, -1.0, ALU.add, 1e30, ALU.mult)
                P.tt(mre[:, :], mre[:, :], t4[:, :], ALU.add)
                P.memset(mseq[:, 0:1], NEG)
                P.scan(mseq[:, 1:NC + 1], fte[:, :], mre[:, :], NEG, ALU.add, ALU.max)
                P.tt(t4[:, :], fte[:, :], mseq[:, 0:NC], ALU.add)
                P.tt(t4[:, :], t4[:, :], mseq[:, 1:NC + 1], ALU.subtract)
                P.act(V(ab.t[:, :, 0], ab.b), t4[:, :], AF.Exp)
                P.tt(t4[:, :], mre[:, :], mseq[:, 1:NC + 1], ALU.subtract)
                P.act(V(ab.t[:, :, 1], ab.b), t4[:, :], AF.Exp)
                P.tt(V(ab.t[:, :, 1], ab.b), V(ab.t[:, :, 1], ab.b), sel4, ALU.mult)
                abb = [P.sb(f"abb{h}", [128, NC * 2], F32) for h in range(NH)]
                for h in range(NH):
                    P.mm(PSB[6][:, 0:NC * 2], g4[:, h * 128:(h + 1) * 128],
                         V(ab.t[:, :, :].rearrange("p r c -> p (r c)"), ab.b), f32=True)
                    P.cp(abb[h][:, :], PSB[6][:, 0:NC * 2], eng="act")
                P.cp(st["m0"][:, 0:1], mseq[:, NC:NC + 1])
                for h in range(NH):
                    P.memset(st["S"][h][:, :], 0.0)
                    P.memset(st["Cn"][h][:, :], 0.0)
                xr = [P.sb(f"xr{i}", [128, 8, XW], F32) for i in range(2)]
                de = P.sb("de", [128, 1], F32)
                tS = P.sb("tS", [128, DV + 1], F32)
                for r in range(NC):
                    a = r % 2
                    P.dma(xr[a][:, :, :], V(xa[r, 0:1024, :].rearrange("(g p) w -> p g w", p=128), None))
                    for h in range(NH):
                        P.ts(de[:, :], xr[a][:, h, DV:DV + 1], -1.0, ALU.add, C_("sel", r), ALU.mult)
                        P.ts(de[:, :], de[:, :], 1.0, ALU.add)
                        P.ts(tS[:, 0:DV], xr[a][:, h, 0:DV], C_("sel", r), ALU.mult)
                        P.stt(st["S"][h][:, :], st["S"][h][:, :], de[:, 0:1], tS[:, 0:DV], ALU.mult, ALU.add)
                        P.ts(tS[:, :], xr[a][:, 4 + h, 0:DV + 1], abb[h][:, 2 * r + 1:2 * r + 2], ALU.mult)
                        P.stt(st["Cn"][h][:, :], st["Cn"][h][:, :], abb[h][:, 2 * r:2 * r + 1], tS[:, :],
                              ALU.mult, ALU.add)
                for h in range(NH):
                    P.cp(st["Sb"][h][:, :], st["S"][h][:, :], eng="act")
                    P.cp(st["Cb"][h][:, :], st["Cn"][h][:, 0:DV], eng="act")
                    P.ts(st["nbc"][h][:, :], ones32, st["Cn"][h][:, DV:DV + 1], ALU.mult)
            mixer_pass(l, True, st)

    def phase_merge(l):
        xsrc = x_in if l == 0 else xs
        with P.phase():
            aT = P.sb("yT", [128, KC, TT], BF16)
            ob = P.sb("ob", [128, 24, TT], BF16)
            gl32 = P.sb("gl32", [128, 2, TT], F32)
            glb = P.sb("glb", [128, 2, TT], BF16)
            gt = [P.sb(f"gt{i}", [128, TT], F32) for i in range(2)]
            y32 = [P.sb(f"y32{i}", [128, TT], F32) for i in range(2)]
            t32 = [P.sb(f"t32{i}", [128, TT], F32) for i in range(2)]
            xc = [P.sb(f"xc{i}", [128, TT], F32) for i in range(3)]
            wsl = [P.sb(f"wG{i}", [128, 32 * 256], BF16) for i in range(2)]
            ev = 0
            xi = 0
            for j in range(NT):
                tsl = slice(j * TT, (j + 1) * TT)
                P.dma(ob[:, :, :], V(obr[:, tsl].rearrange("(k p) t -> p k t", p=128), None))
                P.dma(gl32[:, :, :], V(zf[5120:5376, tsl].rearrange("(k p) t -> p k t", p=128), None))
                P.cp(glb[:, :, :], gl32[:, :, :], eng="act")
                for p0 in range(0, D, 256):
                    wv = load_panel(wsl, l, "w_bg", 0, 30, p0, 256)
                    for m0 in range(0, 256, 128):
                        oc = (p0 + m0) // 128
                        yy = y32[oc % 2]
                        for br in range(3):
                            pg = PSB[ev % 6]
                            ev += 1
                            for k in range(2):
                                P.mm(pg[:, 0:TT], V(wv.ap[:, 24 + br * 2 + k, m0:m0 + 128], wv.b), glb[:, k, :],
                                     start=(k == 0), stop=(k == 1))
                            g_ = gt[ev % 2]
                            P.act(g_[:, :], pg[:, 0:TT], AF.Sigmoid, bias=C_("b_gate", (l * 3 + br) * KC + oc))
                            pb = PSB[ev % 6]
                            ev += 1
                            for k in range(8):
                                P.mm(pb[:, 0:TT], V(wv.ap[:, br * 8 + k, m0:m0 + 128], wv.b), ob[:, br * 8 + k, :],
                                     start=(k == 0), stop=(k == 7))
                            if br == 0:
                                P.tt(yy[:, :], g_[:, :], pb[:, 0:TT], ALU.mult)
                            elif br == 1:
                                tt_ = t32[oc % 2]
                                P.tt(tt_[:, :], g_[:, :], pb[:, 0:TT], ALU.mult)
                                P.tt(yy[:, :], yy[:, :], tt_[:, :], ALU.add)
                            else:
                                tt_ = t32[oc % 2]
                                P.tt(tt_[:, :], g_[:, :], pb[:, 0:TT], ALU.mult)
                                P.tt(aT[:, oc, :], yy[:, :], tt_[:, :], ALU.add)
                for p0 in range(0, D, 256):
                    wv = load_panel(wsl, l, "w_out", 0, KC, p0, 256)
                    for m0 in range(0, 256, 128):
                        oc = (p0 + m0) // 128
                        x_ = xc[xi % 3]
                        xi += 1
                        P.dma(x_[:, :], V(xsrc[oc * 128:(oc + 1) * 128, tsl], None))
                        ps = PSB[ev % 6]
                        ev += 1
                        for k in range(KC):
                            P.mm(ps[:, 0:TT], V(wv.ap[:, k, m0:m0 + 128], wv.b), aT[:, k, :],
                                 start=(k == 0), stop=(k == KC - 1))
                        P.tt(x_[:, :], x_[:, :], ps[:, 0:TT], ALU.add)
                        P.dma(V(xs[oc * 128:(oc + 1) * 128, tsl], None), x_[:, :])

    def phase_ffn(l):
        last = (l == L - 1)
        with P.phase():
            xt = P.sb("xt", [128, KC, TT], F32)
            aT = P.sb("h2T", [128, KC, TT], BF16)
            t32 = [P.sb(f"r32{i}", [128, TT], F32) for i in range(2)]
            aseg = P.sb("aseg", [128, SEG, TT], BF16)
            wsl = [P.sb(f"wF{i}", [128, 32 * 256], BF16) for i in range(2)]
            sq = P.sb("fsq", [128, TT], F32)
            acc = P.sb("facc", [128, TT], F32)
            std = P.sb("fstd", [128, TT], F32)
            rstd = P.sb("frstd", [128, TT], F32)
            ev = 0
            for j in range(NT):
                tsl = slice(j * TT, (j + 1) * TT)
                P.dma(xt[:, :, :], V(xs[:, tsl].rearrange("(k p) t -> p k t", p=128), None))
                for k in range(KC):
                    if k == 0:
                        P.act(acc[:, :], xt[:, k, :], AF.Square)
                    else:
                        P.act(sq[:, :], xt[:, k, :], AF.Square)
                        P.tt(acc[:, :], acc[:, :], sq[:, :], ALU.add)
                P.mm(PSB[6][:, 0:TT], ones32, acc[:, :], f32=True)
                rms_scale(PSB[6][:, 0:TT], std[:, :], rstd[:, :], D, TT)
                for k in range(KC):
                    P.stt(aT[:, k, :], xt[:, k, :], C_("norm_ffn", l * KC + k), rstd[:, :], ALU.mult, ALU.mult)
                for sgi in range(NSEG):
                    for p0 in range(0, SEG * 128, 256):
                        wv = load_panel(wsl, l, "w_ff1", 0, KC, sgi * SEG * 128 + p0, 256)
                        for m0 in range(0, 256, 128):
                            ci = (p0 + m0) // 128
                            ps = PSB[ev % 6]
                            ev += 1
                            for k in range(KC):
                                P.mm(ps[:, 0:TT], V(wv.ap[:, k, m0:m0 + 128], wv.b), aT[:, k, :],
                                     start=(k == 0), stop=(k == KC - 1))
                            r_ = t32[ci % 2]
                            P.act(r_[:, :], ps[:, 0:TT], AF.Relu)
                            P.tt(aseg[:, ci, :], r_[:, :], r_[:, :], ALU.mult)
                    for p0 in range(0, D, 512):
                        pw = min(512, D - p0)
                        wv = load_panel(wsl, l, "w_ff2", sgi * SEG * 128, SEG, p0, pw)
                        for m0 in range(0, pw, 128):
                            oc = (p0 + m0) // 128
                            ps = PSB[ev % 6]
                            ev += 1
                            for k in range(SEG):
                                P.mm(ps[:, 0:TT], V(wv.ap[:, k, m0:m0 + 128], wv.b), aseg[:, k, :],
                                     start=(k == 0), stop=(k == SEG - 1))
                            P.tt(xt[:, oc, :], xt[:, oc, :], ps[:, 0:TT], ALU.add)
                if not last:
                    P.dma(V(xs[:, tsl].rearrange("(k p) t -> p k t", p=128), None), xt[:, :, :])
                else:
                    for k in range(KC):
                        if k == 0:
                            P.act(acc[:, :], xt[:, k, :], AF.Square)
                        else:
                            P.act(sq[:, :], xt[:, k, :], AF.Square)
                            P.tt(acc[:, :], acc[:, :], sq[:, :], ALU.add)
                    P.mm(PSB[6][:, 0:TT], ones32, acc[:, :], f32=True)
                    rms_scale(PSB[6][:, 0:TT], std[:, :], rstd[:, :], D, TT)
                    for k in range(KC):
                        P.stt(xt[:, k, :], xt[:, k, :], C_("final_norm", k), rstd[:, :], ALU.mult, ALU.mult)
                    P.dma(V(outT[:, tsl].rearrange("(k p) t -> p k t", p=128), None), xt[:, :, :])

    kmT = P.sb("kmT", [128, 8, MEM], BF16)
    vm = P.sb("vm", [128, 2, MIXV], BF16)
    for l in range(L):
        phase_inproj(l)
        phase_memkv(l, kmT, vm)
        phase_memattn(l, kmT, vm)
        phase_conv(l)
        phase_mixers(l)
        phase_merge(l)
        phase_ffn(l)
    P.barrier()
    for e in ENGS:
        P.add(e, None)

    P.finalize(eng_sems)
    with nc.Block() as block:
        @block.tensor
        def _(e):
            P.emit("pe", e, eng_sems)

        @block.scalar
        def _(e):
            P.emit("act", e, eng_sems)

        @block.vector
        def _(e):
            P.emit("dve", e, eng_sems)

        @block.gpsimd
        def _(e):
            P.emit("pool", e, eng_sems)

        @block.sync
        def _(e):
            P.emit("sp", e, eng_sems)
    P.top.close()
    return nc


def _cols(vec):
    v = np.asarray(vec, np.float32).reshape(-1, 128)
    return np.ascontiguousarray(v.T)


def make_in_maps(inp, NC, T, D, DFF, L):
    KC = D // 128
    f32 = np.float32
    x = np.asarray(inp["x"], f32)[0]
    mem = np.asarray(inp["mem"], f32)[0]
    memT = np.ascontiguousarray(mem.T)
    parts = []
    parts.append(np.concatenate([_cols(inp["norm_mix"][l]) for l in range(L)], 1))
    parts.append(np.concatenate([_cols(inp["norm_ffn"][l]) for l in range(L)], 1))
    parts.append(_cols(inp["final_norm"]))
    parts.append(_cols(inp["mem_norm"]))
    parts.append(np.concatenate([_cols(inp["b_gate"][l][b]) for l in range(L) for b in range(3)], 1))
    parts.append(np.concatenate([_cols(inp["gla_norm"][l]) for l in range(L)], 1))
    parts.append(np.concatenate([_cols(inp["ml_norm"][l]) for l in range(L)], 1))
    parts.append(np.concatenate([_cols(-np.asarray(inp["b_gla_a"][l], f32)) for l in range(L)], 1))
    parts.append(np.concatenate([_cols(inp["conv_w"][l][j]) for l in range(L) for j in range(4)], 1))
    parts.append(np.concatenate([_cols(inp["conv_b"][l]) for l in range(L)], 1))
    cols_common = np.concatenate(parts, 1).astype(f32)
    consts = np.zeros((128, 384), f32)
    consts[:, 0:128] = np.triu(np.ones((128, 128), f32))
    consts[:, 128:256] = np.eye(128, dtype=f32)
    consts[:, 256:384] = 1.0
    g4c = np.zeros((4, 4 * 128 + 4 * L + 8), f32)
    for h in range(4):
        g4c[h, h * 128:(h + 1) * 128] = 1.0
    g4m = np.zeros((4, 2 * T), f32)
    rst = np.ones(T, f32)
    rst[0::BLK] = 0.0
    g4m[:, 0:T] = rst[None]
    rneg = np.zeros(T, f32)
    rneg[0::BLK] = NEG
    g4m[:, T:2 * T] = rneg[None]
    for l in range(L):
        g4c[:, 512 + 4 * l + 0] = -np.asarray(inp["b_ml_f"][l], f32)
        g4c[:, 512 + 4 * l + 1] = np.asarray(inp["b_ml_i"][l], f32)
    wga = np.concatenate([np.asarray(inp["w_gla_a"][l], f32) for l in range(L)], 1)
    wmats = {}
    for l in range(L):
        wmats[(l, "w_in")] = np.asarray(inp["w_in"][l], f32)
        wmats[(l, "w_mk")] = np.asarray(inp["w_mem_k"][l], f32)
        wmats[(l, "w_mv")] = np.asarray(inp["w_mem_v"][l], f32)
        wmats[(l, "w_bg")] = np.concatenate([np.asarray(inp["w_branch"][l], f32).reshape(3 * MIXV, D),
                                             np.asarray(inp["w_gate"][l], f32).reshape(3 * 256, D)], 0)
        wmats[(l, "w_out")] = np.asarray(inp["w_out"][l], f32)
        wmats[(l, "w_ff1")] = np.asarray(inp["w_ff1"][l], f32)
        wmats[(l, "w_ff2")] = np.asarray(inp["w_ff2"][l], f32)
    maps = []
    for c in range(NC):
        m = {}
        m["xT"] = np.ascontiguousarray(x[c * T:(c + 1) * T].T)
        m["memT"] = memT
        sel = np.zeros((128, 8), f32)
        sel[:, :c] = 1.0
        selp = np.zeros((128, 8), f32)
        if c > 0:
            selp[:, c - 1] = 1.0
        m["cols"] = np.ascontiguousarray(np.concatenate([cols_common, sel, selp], 1))
        m["consts"] = consts
        g = g4c.copy()
        g[:, 512 + 4 * L:512 + 4 * L + 8] = sel[:4, :]
        m["g4"] = g
        m["g4m"] = g4m
        m["wga"] = np.ascontiguousarray(wga)
        for (l, wn), w in wmats.items():
            R = w.shape[0] // NC
            m[f"{wn}_{l}"] = np.ascontiguousarray(w[c * R:(c + 1) * R])
        maps.append(m)
    return maps


_NC_CACHE = {}


def run(inp, NC, T, D, DFF, L, debug=False):
    key = (NC, T, D, DFF, L, debug)
    if key not in _NC_CACHE:
        _NC_CACHE[key] = build_program(NC, T, D, DFF, L, debug=debug)
    nc = _NC_CACHE[key]
    maps = make_in_maps(inp, NC, T, D, DFF, L)
    res = run_bass_kernel_spmd(nc, maps, core_ids=list(range(NC)))
    out = np.concatenate([np.asarray(r["outT"]).T for r in res.results], 0)[None]
    return out.astype(np.float32), res


def kernel(**inputs):
    out, _ = run(inputs, 8, 2048, 4096, 16384, 4)
    return out
```
